# Optimizing a Trainium2 kernel written in Bass

```python
import math
import jax, jax.numpy as jnp
from jax import lax
import numpy as np

D_MODEL = 2048
BATCH = 4
SEQ = 2048
DEPTH = 2
DEC_BATCH = 128
DEC_SEQ = 8
PAST_LEN = 16384
PAGE_SIZE = 128

MIX_WIDTH = D_MODEL
N_MIXERS = 4
GROUP_WIDTH = MIX_WIDTH // N_MIXERS
POOL_WINDOWS = (2, 4, 8, 16)
POOL_GROUP = GROUP_WIDTH // len(POOL_WINDOWS)
POOL_BUF = max(POOL_WINDOWS) - 1
SG_CHUNK = 128
SG_HEADS = 4
SG_HEAD_DIM = GROUP_WIDTH // SG_HEADS
SSM_IN = 16
SSM_GROUPS = GROUP_WIDTH // SSM_IN
SSM_STATE = 64
CONV_WIDTH = 31
CONV_BUF = CONV_WIDTH - 1
D_FF = (11 * D_MODEL) // 4
IN_COLS = 6 * GROUP_WIDTH
SPLITS = tuple(GROUP_WIDTH * i for i in range(1, 6))
EPS = 1e-6
DT_MIN = 1e-3
DT_MAX = 1e-1

kernel_name = "hybrid_pool_gmlp_s5_conv_decode_step"


def rms_norm(x, g):
    xf = x.astype(jnp.float32)
    y = xf * lax.rsqrt(jnp.mean(xf * xf, axis=-1, keepdims=True) + EPS)
    return (y * g.astype(jnp.float32)).astype(x.dtype)


def group_rms_norm(x, g):
    N, L, _ = x.shape
    xf = x.astype(jnp.float32).reshape(N, L, N_MIXERS, GROUP_WIDTH)
    y = xf * lax.rsqrt(jnp.mean(xf * xf, axis=-1, keepdims=True) + EPS)
    return (y.reshape(N, L, MIX_WIDTH) * g.astype(jnp.float32)).astype(x.dtype)


def layer_norm(x, g, b):
    xf = x.astype(jnp.float32)
    mu = jnp.mean(xf, axis=-1, keepdims=True)
    var = jnp.mean(jnp.square(xf - mu), axis=-1, keepdims=True)
    y = (xf - mu) * lax.rsqrt(var + EPS)
    return (y * g.astype(jnp.float32) + b.astype(jnp.float32)).astype(x.dtype)


def swiglu(x, wg, wu, wd):
    return (jax.nn.silu(x @ wg) * (x @ wu)) @ wd


def pool_mixer(p, buf, start, w_grp, scale):
    N, L, _ = p.shape
    ext = jnp.concatenate([buf.astype(p.dtype), p], axis=1)
    cs = jnp.cumsum(ext.astype(jnp.float32), axis=1)
    cs = jnp.pad(cs, ((0, 0), (1, 0), (0, 0)))
    pos = start + jnp.arange(L)
    means = []
    for gi, w in enumerate(POOL_WINDOWS):
        lo, hi = gi * POOL_GROUP, (gi + 1) * POOL_GROUP
        s = cs[:, POOL_BUF + 1:POOL_BUF + 1 + L, lo:hi] - cs[:, POOL_BUF + 1 - w:POOL_BUF + 1 - w + L, lo:hi]
        cnt = jnp.minimum(w, pos + 1).astype(jnp.float32)
        means.append(s / cnt[None, :, None])
    mean = jnp.concatenate(means, axis=-1).astype(p.dtype)
    z = (mean - p).reshape(N, L, len(POOL_WINDOWS), POOL_GROUP)
    z = jnp.einsum('nlgc,gcd->nlgd', z, w_grp).reshape(N, L, GROUP_WIDTH)
    return z * scale, ext[:, -POOL_BUF:]


def spatial_gate(u, v, w_s, b_s):
    N, L, _ = u.shape
    mask = jnp.tril(jnp.ones((SG_CHUNK, SG_CHUNK), dtype=bool))
    w = jnp.where(mask[None], w_s, jnp.zeros_like(w_s))
    if L % SG_CHUNK == 0:
        nc = L // SG_CHUNK
        vc = v.reshape(N, nc, SG_CHUNK, SG_HEADS, SG_HEAD_DIM)
        mix = jnp.einsum('hts,ncshd->ncthd', w, vc) + b_s.T[None, None, :, :, None]
    else:
        vc = v.reshape(N, L, SG_HEADS, SG_HEAD_DIM)
        mix = jnp.einsum('hts,nshd->nthd', w[:, :L, :L], vc) + b_s[:, :L].T[None, :, :, None]
    return u * mix.reshape(N, L, GROUP_WIDTH)


def ssm_mixer(u, s_re, s_im, a_re, a_im, log_dt, b_re, b_im, c_re, c_im, d, w_glu, b_glu):
    f32 = jnp.float32
    N, L, _ = u.shape
    uf = u.astype(f32).reshape(N, L, SSM_GROUPS, SSM_IN)
    dt = jnp.exp(log_dt.astype(f32))[:, None]
    ar, ai = a_re.astype(f32), a_im.astype(f32)
    mag = jnp.exp(ar * dt)
    abar_re, abar_im = mag * jnp.cos(ai * dt), mag * jnp.sin(ai * dt)
    den = ar * ar + ai * ai
    nr, ni = abar_re - 1.0, abar_im
    coef_re = (nr * ar + ni * ai) / den
    coef_im = (ni * ar - nr * ai) / den
    br, bi = b_re.astype(f32), b_im.astype(f32)
    bbar_re = coef_re[..., None] * br - coef_im[..., None] * bi
    bbar_im = coef_re[..., None] * bi + coef_im[..., None] * br
    bu_re = jnp.einsum('nlgh,gph->nlgp', uf, bbar_re)
    bu_im = jnp.einsum('nlgh,gph->nlgp', uf, bbar_im)
    sr, si = s_re.astype(f32), s_im.astype(f32)
    bu_re = bu_re.at[:, 0].add(abar_re * sr - abar_im * si)
    bu_im = bu_im.at[:, 0].add(abar_re * si + abar_im * sr)
    shp = bu_re.shape
    elems = (jnp.broadcast_to(abar_re, shp), jnp.broadcast_to(abar_im, shp), bu_re, bu_im)

    def combine(e1, e2):
        a1r, a1i, b1r, b1i = e1
        a2r, a2i, b2r, b2i = e2
        return (a2r * a1r - a2i * a1i, a2r * a1i + a2i * a1r,
                a2r * b1r - a2i * b1i + b2r, a2r * b1i + a2i * b1r + b2i)

    _, _, x_re, x_im = lax.associative_scan(combine, elems, axis=1)
    y = (jnp.einsum('nlgp,ghp->nlgh', x_re, c_re.astype(f32))
         - jnp.einsum('nlgp,ghp->nlgh', x_im, c_im.astype(f32)))
    y = y.reshape(N, L, GROUP_WIDTH) + d.astype(f32) * u.astype(f32)
    g = jax.nn.gelu(y).astype(u.dtype)
    out = g * jax.nn.sigmoid(g @ w_glu + b_glu)
    return out, x_re[:, -1].astype(s_re.dtype), x_im[:, -1].astype(s_im.dtype)


def conv_mixer(a, gate, buf, w_dw, b_dw, ln_g, ln_b, w_pw):
    h = a * jax.nn.sigmoid(gate)
    ext = jnp.concatenate([buf.astype(h.dtype), h], axis=1)
    y = lax.conv_general_dilated(ext, w_dw[:, None, :], window_strides=(1,), padding='VALID',
                                 dimension_numbers=('NWC', 'WIO', 'NWC'),
                                 feature_group_count=GROUP_WIDTH) + b_dw
    y = layer_norm(y, ln_g, ln_b)
    y = jax.nn.silu(y) @ w_pw
    return y, ext[:, -CONV_BUF:]


def decoder_layer(x, start, pool_buf, conv_buf, s_re, s_im, lp):
    h = x + 0.5 * swiglu(rms_norm(x, lp['ffn1_norm']), lp['ffn1_w_gate'], lp['ffn1_w_up'], lp['ffn1_w_down'])
    n = rms_norm(h, lp['mix_norm'])
    proj = n @ lp['w_in']
    p_pool, sg_u, sg_v, p_ssm, conv_a, conv_g = jnp.split(proj, SPLITS, axis=-1)
    o_pool, new_pool = pool_mixer(p_pool, pool_buf, start, lp['pool_w'], lp['pool_scale'])
    o_sg = spatial_gate(sg_u, sg_v, lp['sg_w'], lp['sg_b'])
    o_ssm, new_re, new_im = ssm_mixer(p_ssm, s_re, s_im, lp['ssm_a_re'], lp['ssm_a_im'], lp['ssm_log_dt'],
                                      lp['ssm_b_re'], lp['ssm_b_im'], lp['ssm_c_re'], lp['ssm_c_im'],
                                      lp['ssm_d'], lp['ssm_w_glu'], lp['ssm_b_glu'])
    o_conv, new_conv = conv_mixer(conv_a, conv_g, conv_buf, lp['conv_w'], lp['conv_b'],
                                  lp['conv_ln_g'], lp['conv_ln_b'], lp['conv_w_pw'])
    mixed = group_rms_norm(jnp.concatenate([o_pool, o_sg, o_ssm, o_conv], axis=-1), lp['out_norm_g'])
    h = h + mixed @ lp['w_out']
    h = h + 0.5 * swiglu(rms_norm(h, lp['ffn2_norm']), lp['ffn2_w_gate'], lp['ffn2_w_up'], lp['ffn2_w_down'])
    return h, new_pool, new_conv, new_re, new_im, sg_v


def setup_inputs(seed: int = 0) -> dict:
    key = jax.random.key(seed)
    ks = iter(jax.random.split(key, 48))
    f32 = jnp.float32

    def nrm(shape, scale):
        return jax.random.normal(next(ks), shape, f32) * scale

    def gain(shape):
        return 1.0 + nrm(shape, 0.02)

    n_idx = jnp.arange(SSM_STATE, dtype=f32)
    inp = {}
    inp['x_prompt'] = nrm((BATCH, SEQ, D_MODEL), 1.0)
    inp['x_sample'] = nrm((DEC_BATCH, DEC_SEQ, D_MODEL), 1.0)
    inp['state_pool'] = nrm((DEPTH, DEC_BATCH, POOL_BUF, GROUP_WIDTH), 1.0)
    inp['state_conv'] = nrm((DEPTH, DEC_BATCH, CONV_BUF, GROUP_WIDTH), 0.5)
    inp['state_ssm_re'] = nrm((DEPTH, DEC_BATCH, SSM_GROUPS, SSM_STATE), 0.1)
    inp['state_ssm_im'] = nrm((DEPTH, DEC_BATCH, SSM_GROUPS, SSM_STATE), 0.1)
    inp['ffn1_norm'] = gain((DEPTH, D_MODEL))
    inp['ffn1_w_gate'] = nrm((DEPTH, D_MODEL, D_FF), D_MODEL ** -0.5)
    inp['ffn1_w_up'] = nrm((DEPTH, D_MODEL, D_FF), D_MODEL ** -0.5)
    inp['ffn1_w_down'] = nrm((DEPTH, D_FF, D_MODEL), D_FF ** -0.5)
    inp['mix_norm'] = gain((DEPTH, D_MODEL))
    inp['w_in'] = nrm((DEPTH, D_MODEL, IN_COLS), D_MODEL ** -0.5)
    inp['pool_w'] = nrm((DEPTH, len(POOL_WINDOWS), POOL_GROUP, POOL_GROUP), POOL_GROUP ** -0.5)
    inp['pool_scale'] = 1.0 + nrm((DEPTH, GROUP_WIDTH), 0.1)
    inp['sg_w'] = nrm((DEPTH, SG_HEADS, SG_CHUNK, SG_CHUNK), SG_CHUNK ** -0.5)
    inp['sg_b'] = 1.0 + nrm((DEPTH, SG_HEADS, SG_CHUNK), 0.1)
    inp['ssm_a_re'] = -0.5 + nrm((DEPTH, SSM_GROUPS, SSM_STATE), 0.01)
    inp['ssm_a_im'] = math.pi * n_idx + nrm((DEPTH, SSM_GROUPS, SSM_STATE), 0.01)
    inp['ssm_log_dt'] = jax.random.uniform(next(ks), (DEPTH, SSM_GROUPS), f32,
                                           math.log(DT_MIN), math.log(DT_MAX))
    inp['ssm_b_re'] = nrm((DEPTH, SSM_GROUPS, SSM_STATE, SSM_IN), (2 * SSM_IN) ** -0.5)
    inp['ssm_b_im'] = nrm((DEPTH, SSM_GROUPS, SSM_STATE, SSM_IN), (2 * SSM_IN) ** -0.5)
    inp['ssm_c_re'] = nrm((DEPTH, SSM_GROUPS, SSM_IN, SSM_STATE), (2 * SSM_STATE) ** -0.5)
    inp['ssm_c_im'] = nrm((DEPTH, SSM_GROUPS, SSM_IN, SSM_STATE), (2 * SSM_STATE) ** -0.5)
    inp['ssm_d'] = nrm((DEPTH, GROUP_WIDTH), 1.0)
    inp['ssm_w_glu'] = nrm((DEPTH, GROUP_WIDTH, GROUP_WIDTH), GROUP_WIDTH ** -0.5)
    inp['ssm_b_glu'] = nrm((DEPTH, GROUP_WIDTH), 0.02)
    inp['conv_w'] = nrm((DEPTH, CONV_WIDTH, GROUP_WIDTH), CONV_WIDTH ** -0.5)
    inp['conv_b'] = nrm((DEPTH, GROUP_WIDTH), 0.02)
    inp['conv_ln_g'] = gain((DEPTH, GROUP_WIDTH))
    inp['conv_ln_b'] = nrm((DEPTH, GROUP_WIDTH), 0.02)
    inp['conv_w_pw'] = nrm((DEPTH, GROUP_WIDTH, GROUP_WIDTH), GROUP_WIDTH ** -0.5)
    inp['out_norm_g'] = gain((DEPTH, MIX_WIDTH))
    inp['w_out'] = nrm((DEPTH, MIX_WIDTH, D_MODEL), MIX_WIDTH ** -0.5)
    inp['ffn2_norm'] = gain((DEPTH, D_MODEL))
    inp['ffn2_w_gate'] = nrm((DEPTH, D_MODEL, D_FF), D_MODEL ** -0.5)
    inp['ffn2_w_up'] = nrm((DEPTH, D_MODEL, D_FF), D_MODEL ** -0.5)
    inp['ffn2_w_down'] = nrm((DEPTH, D_FF, D_MODEL), D_FF ** -0.5)
    inp['final_norm'] = gain((D_MODEL,))
    return inp


def reference(x_prompt, x_sample, state_pool, state_conv, state_ssm_re, state_ssm_im,
              ffn1_norm, ffn1_w_gate, ffn1_w_up, ffn1_w_down, mix_norm, w_in,
              pool_w, pool_scale, sg_w, sg_b,
              ssm_a_re, ssm_a_im, ssm_log_dt, ssm_b_re, ssm_b_im, ssm_c_re, ssm_c_im,
              ssm_d, ssm_w_glu, ssm_b_glu,
              conv_w, conv_b, conv_ln_g, conv_ln_b, conv_w_pw,
              out_norm_g, w_out, ffn2_norm, ffn2_w_gate, ffn2_w_up, ffn2_w_down, final_norm):
    hp, hs = x_prompt, x_sample
    zero_pool = jnp.zeros((BATCH, POOL_BUF, GROUP_WIDTH), x_prompt.dtype)
    zero_conv = jnp.zeros((BATCH, CONV_BUF, GROUP_WIDTH), x_prompt.dtype)
    zero_ssm = jnp.zeros((BATCH, SSM_GROUPS, SSM_STATE), state_ssm_re.dtype)
    pool_p, pool_s, conv_p, conv_s = [], [], [], []
    re_p, im_p, re_s, im_s, v_s = [], [], [], [], []
    for l in range(DEPTH):
        lp = dict(ffn1_norm=ffn1_norm[l], ffn1_w_gate=ffn1_w_gate[l], ffn1_w_up=ffn1_w_up[l],
                  ffn1_w_down=ffn1_w_down[l], mix_norm=mix_norm[l], w_in=w_in[l],
                  pool_w=pool_w[l], pool_scale=pool_scale[l], sg_w=sg_w[l], sg_b=sg_b[l],
                  ssm_a_re=ssm_a_re[l], ssm_a_im=ssm_a_im[l], ssm_log_dt=ssm_log_dt[l],
                  ssm_b_re=ssm_b_re[l], ssm_b_im=ssm_b_im[l], ssm_c_re=ssm_c_re[l], ssm_c_im=ssm_c_im[l],
                  ssm_d=ssm_d[l], ssm_w_glu=ssm_w_glu[l], ssm_b_glu=ssm_b_glu[l],
                  conv_w=conv_w[l], conv_b=conv_b[l], conv_ln_g=conv_ln_g[l], conv_ln_b=conv_ln_b[l],
                  conv_w_pw=conv_w_pw[l], out_norm_g=out_norm_g[l], w_out=w_out[l],
                  ffn2_norm=ffn2_norm[l], ffn2_w_gate=ffn2_w_gate[l], ffn2_w_up=ffn2_w_up[l],
                  ffn2_w_down=ffn2_w_down[l])
        hp, npool, nconv, nre, nim, _ = decoder_layer(hp, 0, zero_pool, zero_conv, zero_ssm, zero_ssm, lp)
        pool_p.append(npool); conv_p.append(nconv); re_p.append(nre); im_p.append(nim)
        hs, npool, nconv, nre, nim, nv = decoder_layer(hs, PAST_LEN, state_pool[l], state_conv[l],
                                                       state_ssm_re[l], state_ssm_im[l], lp)
        pool_s.append(npool); conv_s.append(nconv); re_s.append(nre); im_s.append(nim); v_s.append(nv)
    y_prompt = rms_norm(hp, final_norm)
    y_sample = rms_norm(hs, final_norm)
    return (y_prompt, y_sample,
            jnp.stack(pool_p), jnp.stack(pool_s),
            jnp.stack(conv_p), jnp.stack(conv_s),
            jnp.stack(re_p), jnp.stack(im_p),
            jnp.stack(re_s), jnp.stack(im_s),
            jnp.stack(v_s))
```

```python
import numpy as np
import concourse.bass as bass
import concourse.mybir as mybir
from concourse.bass_utils import run_bass_kernel_spmd

F32 = mybir.dt.float32
BF16 = mybir.dt.bfloat16
AF = mybir.ActivationFunctionType
ALU = mybir.AluOpType

NT, NP, NS = 1152, 1024, 128
TB = [(0, 384), (384, 768), (768, 1152)]
D, DFF = 2048, 5632
EPS = 1e-6
PI = float(np.pi)

WNAMES = ['ffn1_norm', 'ffn1_w_gate', 'ffn1_w_up', 'ffn1_w_down', 'mix_norm', 'w_in', 'pool_w', 'pool_scale',
          'sg_w', 'sg_b', 'ssm_a_re', 'ssm_a_im', 'ssm_log_dt', 'ssm_b_re', 'ssm_b_im', 'ssm_c_re', 'ssm_c_im',
          'ssm_d', 'ssm_w_glu', 'ssm_b_glu', 'conv_w', 'conv_b', 'conv_ln_g', 'conv_ln_b', 'conv_w_pw',
          'out_norm_g', 'w_out', 'ffn2_norm', 'ffn2_w_gate', 'ffn2_w_up', 'ffn2_w_down', 'final_norm']
WSHAPES = {
    'ffn1_norm': [2, 2048], 'ffn1_w_gate': [2, 2048, 5632], 'ffn1_w_up': [2, 2048, 5632],
    'ffn1_w_down': [2, 5632, 2048], 'mix_norm': [2, 2048], 'w_in': [2, 2048, 3072], 'pool_w': [2, 4, 128, 128],
    'pool_scale': [2, 512], 'sg_w': [2, 4, 128, 128], 'sg_b': [2, 4, 128], 'ssm_a_re': [2, 32, 64],
    'ssm_a_im': [2, 32, 64], 'ssm_log_dt': [2, 32], 'ssm_b_re': [2, 32, 64, 16], 'ssm_b_im': [2, 32, 64, 16],
    'ssm_c_re': [2, 32, 16, 64], 'ssm_c_im': [2, 32, 16, 64], 'ssm_d': [2, 512], 'ssm_w_glu': [2, 512, 512],
    'ssm_b_glu': [2, 512], 'conv_w': [2, 31, 512], 'conv_b': [2, 512], 'conv_ln_g': [2, 512],
    'conv_ln_b': [2, 512], 'conv_w_pw': [2, 512, 512], 'out_norm_g': [2, 2048], 'w_out': [2, 2048, 2048],
    'ffn2_norm': [2, 2048], 'ffn2_w_gate': [2, 2048, 5632], 'ffn2_w_up': [2, 2048, 5632],
    'ffn2_w_down': [2, 5632, 2048], 'final_norm': [2048]}
PV = {'ffn1_norm': (0, 16), 'mix_norm': (16, 16), 'out_norm_g': (32, 16), 'ffn2_norm': (48, 16),
      'pool_scale': (64, 4), 'ssm_d': (68, 4), 'ssm_b_glu': (72, 4), 'conv_b': (76, 4), 'conv_ln_g': (80, 4),
      'conv_ln_b': (84, 4)}
PVN = 88


class Sched:
    def __init__(self, nc):
        self.nc = nc
        self.eng = {'pe': nc.tensor, 'act': nc.scalar, 'dve': nc.vector, 'pool': nc.gpsimd, 'sp': nc.sync}
        self.semh = {}
        self.cnt = {}
        for e in self.eng:
            self.semh[e] = nc.semaphore('s_' + e).__enter__()
            self.cnt[e] = 0
        self.waited = {e: {} for e in self.eng}
        self.lastw = {}
        self.readers = {}

    def dsem(self, name):
        if name not in self.semh:
            self.semh[name] = self.nc.semaphore('d_' + name).__enter__()
            self.cnt[name] = 0
        return self.semh[name]

    def _wait(self, e, tag):
        sk, val = tag
        if self.waited[e].get(sk, 0) >= val:
            return
        self.eng[e].wait_ge(self.semh[sk], val)
        self.waited[e][sk] = val

    def _deps(self, e, reads, writes):
        for k in reads:
            w = self.lastw.get(k)
            if w is not None and not (w[0] == e and e == 'pe'):
                self._wait(e, w)
        for k in writes:
            w = self.lastw.get(k)
            if w is not None and w[0] != e:
                self._wait(e, w)
            for r in self.readers.get(k, ()):
                if r[0] != e:
                    self._wait(e, r)

    def _record(self, tag, reads, writes):
        for k in reads:
            self.readers.setdefault(k, []).append(tag)
        for k in writes:
            self.lastw[k] = tag
            self.readers[k] = []

    def op(self, e, fn, reads=(), writes=(), sig=True):
        self._deps(e, reads, writes)
        ins = fn(self.eng[e])
        if sig:
            self.cnt[e] += 1
            ins.then_inc(self.semh[e], 1)
            tag = (e, self.cnt[e])
        else:
            tag = (e, self.cnt[e] + 1)
        self._record(tag, reads, writes)
        return ins

    def dma(self, q, out, in_, reads=(), writes=(), sem='dgen'):
        self._deps(q, reads, writes)
        h = self.dsem(sem)
        self.eng[q].dma_start(out=out, in_=in_).then_inc(h, 16)
        self.cnt[sem] += 16
        self._record((sem, self.cnt[sem]), reads, writes)

    def barrier(self):
        for e in self.eng:
            for sk in list(self.semh):
                if self.cnt[sk] > 0 and sk != e:
                    self._wait(e, (sk, self.cnt[sk]))
        self.lastw = {}
        self.readers = {}


def build_program(dbg=None):
    nc = bass.Bass(target_bir_lowering=False)
    S = Sched(nc)

    def din(name, shape):
        return nc.dram_tensor(name, list(shape), F32, kind="ExternalInput").ap()

    def dout(name, shape):
        return nc.dram_tensor(name, list(shape), F32, kind="ExternalOutput").ap()

    xtok = din('xtok', [NT, D])
    st_pool = din('st_pool', [2, 16, 15, 512])
    st_conv = din('st_conv', [2, 16, 30, 512])
    st_re = din('st_re', [2, 16, 2048])
    st_im = din('st_im', [2, 16, 2048])
    W = {n: din(n, ([2, 128, 128] if (isinstance(dbg, tuple) and n.startswith('ffn') and 'norm' not in n) else WSHAPES[n])) for n in WNAMES}
    c_ident = din('c_ident', [128, 128])
    c_tri = din('c_tri', [128, 128])
    c_mblk = din('c_mblk', [128, 128])
    c_e8 = din('c_e8', [8, 128])
    c_posb = din('c_posb', [128, 16])
    c_modd = din('c_modd', [128, 1])
    c_tv = din('c_tv', [128, 128])
    c_g2m = din('c_g2m', [128, 2])
    c_rm = din('c_rm', [128, 4])

    o_y = dout('o_y', [NT, D])
    o_pool = dout('o_pool', [2, 17, 15, 512])
    o_conv = dout('o_conv', [2, 17, 30, 512])
    o_re = dout('o_re', [2, 17, 2048])
    o_im = dout('o_im', [2, 17, 2048])
    o_sgv = dout('o_sgv', [2, 128, 512])

    xb_in = nc.dram_tensor('xb_in', [128, 212], F32)
    xb_out = nc.dram_tensor('xb_out', [256, 212], F32)
    xb_in2 = nc.dram_tensor('xb_in2', [128, 212], F32)
    xb_out2 = nc.dram_tensor('xb_out2', [256, 212], F32)
    XB = [(xb_in, xb_out), (xb_in2, xb_out2)]

    def sb(name, shape, dt=F32):
        return nc.sbuf_tensor(name, list(shape), dt).__enter__()

    h = sb('h', [128, 16, NT])
    xn = sb('xn', [128, 16, NT], BF16)
    pvec = sb('pvec', [128, 2 * PVN + 16])
    ident = sb('ident', [128, 128])
    tri = sb('tri', [128, 128])
    mblk = sb('mblk', [128, 128])
    e8 = sb('e8', [8, 128])
    posb = sb('posb', [128, 16])
    modd = sb('modd', [128, 1])
    tv = sb('tv', [128, 128])
    g2m = sb('g2m', [128, 2])
    rmk = sb('rmk', [128, 4])
    ones_bf = sb('ones_bf', [128, 128], BF16)
    ones_f = sb('ones_f', [128, 128])
    ARENA_W = 24300
    arena = sb('arena', [128, ARENA_W])
    psum = nc.psum_tensor('ps', [128, 8, 512], F32).__enter__()

    class Arena:
        def __init__(self):
            self.off = 0

        def reset(self):
            self.off = 0

        def alloc(self, shape, dt=F32):
            n = int(np.prod(shape[1:]))
            w = n if dt == F32 else (n + 1) // 2
            a = arena[:, self.off:self.off + w]
            assert self.off + w <= ARENA_W, (self.off, w)
            self.off += w
            if dt != F32:
                a = a.bitcast(dt)[:, 0:n]
            if len(shape) == 3:
                a = a.rearrange("p (a b) -> p a b", b=shape[2])
            elif len(shape) == 4:
                a = a.rearrange("p (a b c) -> p a b c", b=shape[2], c=shape[3])
            elif len(shape) == 5:
                a = a.rearrange("p (a b c d) -> p a b c d", b=shape[2], c=shape[3], d=shape[4])
            return a[0:shape[0]] if shape[0] != 128 else a

    AR = Arena()

    def bank3(base):
        return psum[:, base:base + 3, 0:384]

    def pskeys(base, n=3):
        return ['ps%d' % (base + i) for i in range(n)]

    cp_flip = [0]

    def evac(out, in_, reads, writes):
        cp_flip[0] ^= 1
        if cp_flip[0]:
            S.op('act', lambda e: e.activation(out=out, in_=in_, func=AF.Copy), reads, writes)
        else:
            S.op('dve', lambda e: e.tensor_copy(out=out, in_=in_), reads, writes)

    for t_, src, k in ((ident, c_ident, 'ident'), (tri, c_tri, 'tri'), (mblk, c_mblk, 'mblk'), (e8, c_e8, 'e8'),
                       (posb, c_posb, 'posb'), (modd, c_modd, 'modd'), (tv, c_tv, 'tv'), (g2m, c_g2m, 'g2m'), (rmk, c_rm, 'rmk')):
        S.dma('sp', t_[:], src, writes=[k], sem='const')
    S.op('dve', lambda e: e.memset(ones_bf[:], 1.0), writes=['ones_bf'])
    S.op('dve', lambda e: e.memset(ones_f[:], 1.0), writes=['ones_f'])

    AR.reset()
    prow = AR.alloc([128, 2, 128])
    S.op('dve', lambda e: e.memset(prow, 0.0), writes=['prow'])
    for l in range(2):
        for n_, (o, nch) in PV.items():
            S.dma('sp', prow[o:o + nch, l, :], W[n_][l].rearrange("(c p) -> c p", p=128), writes=['prow'], sem='prow')
    S.dma('sp', prow[PVN:PVN + 16, 1, :], W['final_norm'].rearrange("(c p) -> c p", p=128), writes=['prow'],
          sem='prow')
    for l in range(2):
        nr = PVN + (16 if l == 1 else 0)
        S.op('pe', lambda e: e.transpose(out=psum[:, 6, 0:nr], in_=prow[0:nr, l, :], identity=ident[0:nr, 0:nr]),
             reads=['prow', 'ident'], writes=['ps6'])
        S.op('dve', lambda e: e.tensor_copy(out=pvec[:, l * PVN:l * PVN + nr], in_=psum[:, 6, 0:nr]),
             reads=['ps6'], writes=['pvec'])

    def pv(l, name, c):
        o = PV[name][0]
        return pvec[:, l * PVN + o + c:l * PVN + o + c + 1]

    xt = [AR.alloc([128, D]) for _ in range(2)]
    for i in range(9):
        s_ = i % 2
        S.dma('sp', xt[s_], xtok[i * 128:(i + 1) * 128, :], writes=['xt%d' % s_], sem='xt%d' % s_)
        for cb in range(4):
            bk = (i * 4 + cb) % 4
            for cc in range(4):
                c = cb * 4 + cc
                S.op('pe', lambda e: e.transpose(out=psum[:, bk, cc * 128:(cc + 1) * 128],
                                                 in_=xt[s_][:, c * 128:(c + 1) * 128], identity=ident[:]),
                     reads=['xt%d' % s_, 'ident'], writes=['ps%d' % bk], sig=(cc == 3))
            evac(h[:, cb * 4:cb * 4 + 4, i * 128:(i + 1) * 128],
                 psum[:, bk, :].rearrange("p (a b) -> p a b", b=128), ['ps%d' % bk], ['h%d' % (cb * 4 + j) for j in range(4)])
    S.barrier()

    HK = ['h%d' % c for c in range(16)]
    XK = ['xn%d' % c for c in range(16)]

    def rms_to_xn(l, gname, ar):
        sq = [ar.alloc([128, NT], BF16) for _ in range(2)]
        rstd = ar.alloc([128, NT])
        for c in range(16):
            s_ = c % 2
            S.op('act', lambda e: e.activation(out=sq[s_], in_=h[:, c, :], func=AF.Square),
                 reads=[HK[c]], writes=['sq%d' % s_])
            for tb, (a, b) in enumerate(TB):
                S.op('pe', lambda e: e.matmul(psum[:, tb, 0:384], lhsT=ones_bf[:], rhs=sq[s_][:, a:b],
                                              start=(c == 0), stop=(c == 15)),
                     reads=['sq%d' % s_, 'ones_bf'], writes=['ps%d' % tb], sig=(tb == 2))
        S.op('act', lambda e: e.activation(out=rstd.rearrange("p (a b) -> p a b", b=384), in_=bank3(0),
                                           func=AF.Sqrt, scale=1.0 / D, bias=EPS),
             reads=pskeys(0), writes=['rstd'])
        S.op('dve', lambda e: e.reciprocal(out=rstd, in_=rstd), reads=['rstd'], writes=['rstd'])
        for c in range(16):
            S.op('dve', lambda e: e.scalar_tensor_tensor(out=xn[:, c, :], in0=h[:, c, :], scalar=pv(l, gname, c),
                                                         in1=rstd, op0=ALU.mult, op1=ALU.mult),
                 reads=[HK[c], 'rstd', 'pvec'], writes=[XK[c]])

    def ffn(l, which):
        AR.reset()
        gname = 'ffn%d_norm' % which
        wg, wu, wd = W['ffn%d_w_gate' % which][l], W['ffn%d_w_up' % which][l], W['ffn%d_w_down' % which][l]
        rms_to_xn(l, gname, AR)
        wgs = [AR.alloc([128, 16, 256], BF16) for _ in range(2)]
        wus = [AR.alloc([128, 16, 256], BF16) for _ in range(2)]
        wds = [AR.alloc([128, 4, 512], BF16) for _ in range(3)]
        hT = AR.alloc([128, 4, NT], BF16)
        tmp = [AR.alloc([128, NT], BF16) for _ in range(2)]
        wgv = wg.rearrange("(k p) f -> p k f", p=128)
        wuv = wu.rearrange("(k p) f -> p k f", p=128)
        wdv = wd.rearrange("(k p) d -> p k d", p=128)
        nsl = 0
        nds = 0
        ntmp = 0
        for G in range(11):
            for hf in range(2):
                s_ = nsl % 2
                nsl += 1
                f0 = G * 512 + hf * 256
                S.dma('pool', wgs[s_], wgv[:, :, f0:f0 + 256], writes=['wg%d' % s_], sem='wg%d' % s_)
                S.dma('pool', wus[s_], wuv[:, :, f0:f0 + 256], writes=['wu%d' % s_], sem='wu%d' % s_)
                for fi in range(2):
                    fl = hf * 2 + fi
                    for (wsl, wk, base) in ((wgs, 'wg%d' % s_, 0), (wus, 'wu%d' % s_, 3)):
                        for k in range(16):
                            for tb, (a, b) in enumerate(TB):
                                S.op('pe', lambda e: e.matmul(psum[:, base + tb, 0:384],
                                                              lhsT=wsl[s_][:, k, fi * 128:(fi + 1) * 128],
                                                              rhs=xn[:, k, a:b], start=(k == 0), stop=(k == 15)),
                                     reads=[wk, XK[k]], writes=['ps%d' % (base + tb)], sig=(k == 15 and tb == 2))
                    t_ = ntmp % 2
                    ntmp += 1
                    S.op('act', lambda e: e.activation(out=tmp[t_].rearrange("p (a b) -> p a b", b=384),
                                                       in_=bank3(0), func=AF.Silu),
                         reads=pskeys(0), writes=['tmp%d' % t_])
                    S.op('dve', lambda e: e.tensor_tensor(out=hT[:, fl, :].rearrange("p (a b) -> p a b", b=384),
                                                          in0=bank3(3),
                                                          in1=tmp[t_].rearrange("p (a b) -> p a b", b=384),
                                                          op=ALU.mult),
                         reads=pskeys(3) + ['tmp%d' % t_], writes=['hT%d' % fl])
            for dblk in range(4):
                s_ = nds % 3
                nds += 1
                S.dma('pool', wds[s_], wdv[:, G * 4:G * 4 + 4, dblk * 512:(dblk + 1) * 512], writes=['wd%d' % s_],
                      sem='wd%d' % s_)
                for dd in range(4):
                    d = dblk * 4 + dd
                    base = 0 if d % 2 == 0 else 3
                    for fl in range(4):
                        for tb, (a, b) in enumerate(TB):
                            S.op('pe', lambda e: e.matmul(psum[:, base + tb, 0:384],
                                                          lhsT=wds[s_][:, fl, dd * 128:(dd + 1) * 128],
                                                          rhs=hT[:, fl, a:b], start=(fl == 0), stop=(fl == 3)),
                                 reads=['wd%d' % s_, 'hT%d' % fl], writes=['ps%d' % (base + tb)],
                                 sig=(fl == 3 and tb == 2))
                    S.op('dve', lambda e: e.scalar_tensor_tensor(
                        out=h[:, d, :].rearrange("p (a b) -> p a b", b=384), in0=bank3(base), scalar=0.5,
                        in1=h[:, d, :].rearrange("p (a b) -> p a b", b=384), op0=ALU.mult, op1=ALU.add),
                         reads=pskeys(base) + [HK[d]], writes=[HK[d]])
        S.barrier()

    def final_out():
        AR.reset()
        rstd = None
        sq = [AR.alloc([128, NT], BF16) for _ in range(2)]
        rstd = AR.alloc([128, NT])
        for c in range(16):
            s_ = c % 2
            S.op('act', lambda e: e.activation(out=sq[s_], in_=h[:, c, :], func=AF.Square),
                 reads=[HK[c]], writes=['sq%d' % s_])
            for tb, (a, b) in enumerate(TB):
                S.op('pe', lambda e: e.matmul(psum[:, tb, 0:384], lhsT=ones_bf[:], rhs=sq[s_][:, a:b],
                                              start=(c == 0), stop=(c == 15)),
                     reads=['sq%d' % s_, 'ones_bf'], writes=['ps%d' % tb], sig=(tb == 2))
        S.op('act', lambda e: e.activation(out=rstd.rearrange("p (a b) -> p a b", b=384), in_=bank3(0),
                                           func=AF.Sqrt, scale=1.0 / D, bias=EPS),
             reads=pskeys(0), writes=['rstd'])
        S.op('dve', lambda e: e.reciprocal(out=rstd, in_=rstd), reads=['rstd'], writes=['rstd'])
        fo = 2 * PVN
        for c in range(16):
            S.op('dve', lambda e: e.scalar_tensor_tensor(out=h[:, c, :], in0=h[:, c, :],
                                                         scalar=pvec[:, fo + c:fo + c + 1],
                                                         in1=rstd, op0=ALU.mult, op1=ALU.mult),
                 reads=[HK[c], 'rstd', 'pvec'], writes=[HK[c]])
        yo = [AR.alloc([128, D]) for _ in range(2)]
        nb = 0
        for i in range(9):
            s_ = i % 2
            for cb in range(4):
                bk = nb % 6
                nb += 1
                for cc in range(4):
                    c = cb * 4 + cc
                    S.op('pe', lambda e: e.transpose(out=psum[:, bk, cc * 128:(cc + 1) * 128],
                                                     in_=h[:, c, i * 128:(i + 1) * 128], identity=ident[:]),
                         reads=[HK[c], 'ident'], writes=['ps%d' % bk], sig=(cc == 3))
                evac(yo[s_][:, cb * 512:(cb + 1) * 512], psum[:, bk, :], ['ps%d' % bk], ['yo%d' % s_])
            S.dma('sp', o_y[i * 128:(i + 1) * 128, :], yo[s_], reads=['yo%d' % s_], sem='yo%d' % s_)
        S.barrier()

    def mixers(l):
        stage = dbg[1] if isinstance(dbg, tuple) else 99
        AR.reset()
        rms_to_xn(l, 'mix_norm', AR)
        S.barrier()
        AR.reset()
        winv = W['w_in'][l].rearrange("(k p) f -> p k f", p=128)
        woutv = W['w_out'][l].rearrange("(k p) d -> p k d", p=128)
        wins = AR.alloc([128, 16, 512], BF16)
        om = AR.alloc([128, 4, NT], BF16)
        u_bf = AR.alloc([128, 4, NT], BF16)
        xsb = AR.alloc([128, 212])
        xrv = AR.alloc([128, 212])
        base_off = AR.off
        nwo = [0]

        def v3(ap):
            return ap.rearrange("p (a b) -> p a b", b=384)

        def load_win(col0):
            S.dma('pool', wins, winv[:, :, col0:col0 + 512], writes=['win'], sem='win')

        def proj_fm(m, base, tbs=(0, 1, 2)):
            for k in range(16):
                for tb in tbs:
                    a, b = TB[tb]
                    S.op('pe', lambda e: e.matmul(psum[:, base + tb, 0:384], lhsT=wins[:, k, m * 128:(m + 1) * 128],
                                                  rhs=xn[:, k, a:b], start=(k == 0), stop=(k == 15)),
                         reads=['win', XK[k]], writes=['ps%d' % (base + tb)], sig=(k == 15 and tb == tbs[-1]))

        def ones_sum(src_fn, nch, keys, f32=False):
            for c in range(nch):
                for tb, (a, b) in enumerate(TB):
                    S.op('pe', lambda e: e.matmul(psum[:, tb, 0:384], lhsT=(ones_f[:] if f32 else ones_bf[:]),
                                                  rhs=src_fn(c)[:, a:b], start=(c == 0), stop=(c == nch - 1)),
                         reads=[keys(c), 'ones_bf', 'ones_f'], writes=['ps%d' % tb], sig=(c == nch - 1 and tb == 2))

        def finish(mi):
            m0 = AR.off
            sq = AR.alloc([128, 4, NT], BF16)
            rstd = AR.alloc([128, NT])
            wos = [AR.alloc([128, 4, 512], BF16) for _ in range(2)]
            for c in range(4):
                S.op('act', lambda e: e.activation(out=sq[:, c, :], in_=om[:, c, :], func=AF.Square),
                     reads=['om'], writes=['gsq%d' % c])
            ones_sum(lambda c: sq[:, c, :], 4, lambda c: 'gsq%d' % c)
            S.op('act', lambda e: e.activation(out=v3(rstd), in_=bank3(0), func=AF.Sqrt, scale=1.0 / 512, bias=EPS),
                 reads=pskeys(0), writes=['grstd'])
            S.op('dve', lambda e: e.reciprocal(out=rstd, in_=rstd), reads=['grstd'], writes=['grstd'])
            for c in range(4):
                S.op('dve', lambda e: e.scalar_tensor_tensor(out=om[:, c, :], in0=om[:, c, :],
                                                             scalar=pv(l, 'out_norm_g', mi * 4 + c), in1=rstd,
                                                             op0=ALU.mult, op1=ALU.mult),
                     reads=['om', 'grstd', 'pvec'], writes=['om'])
            for dblk in range(4):
                s_ = nwo[0] % 2
                nwo[0] += 1
                S.dma('pool', wos[s_], woutv[:, mi * 4:mi * 4 + 4, dblk * 512:(dblk + 1) * 512],
                      writes=['wo%d' % s_], sem='wo%d' % s_)
                for dd in range(4):
                    d = dblk * 4 + dd
                    base = 0 if d % 2 == 0 else 3
                    for k in range(4):
                        for tb, (a, b) in enumerate(TB):
                            S.op('pe', lambda e: e.matmul(psum[:, base + tb, 0:384],
                                                          lhsT=wos[s_][:, k, dd * 128:(dd + 1) * 128],
                                                          rhs=om[:, k, a:b], start=(k == 0), stop=(k == 3)),
                                 reads=['wo%d' % s_, 'om'], writes=['ps%d' % (base + tb)], sig=(k == 3 and tb == 2))
                    S.op('dve', lambda e: e.tensor_tensor(out=v3(h[:, d, :]), in0=bank3(base), in1=v3(h[:, d, :]),
                                                          op=ALU.add),
                         reads=pskeys(base) + [HK[d]], writes=[HK[d]])
            S.barrier()
            AR.off = m0

        def tr_out(src_fn, ncols, dst, key, stg, cst):
            for c in range(4):
                src = src_fn(c)
                dstv = cst[:, c, 0:ncols]
                if len(src.shape) == 3:
                    dstv = dstv.rearrange("p (n t) -> p n t", t=src.shape[2])
                S.op('dve', lambda e: e.tensor_copy(out=dstv, in_=src), reads=[key], writes=['cst'])
            for c in range(4):
                S.op('pe', lambda e: e.transpose(out=psum[0:ncols, 7, c * 128:(c + 1) * 128], in_=cst[:, c, 0:ncols],
                                                 identity=ident[:]),
                     reads=['cst', 'ident'], writes=['ps7'], sig=(c == 3))
            S.op('dve', lambda e: e.tensor_copy(out=stg[0:ncols, :], in_=psum[0:ncols, 7, :]), reads=['ps7'],
                 writes=['stg'])
            S.dma('sp', dst, stg[0:ncols, :], reads=['stg'], sem='stg')

        def tr_in(dst_fn, nrows, src, key, stg):
            S.dma('sp', stg[0:nrows, :], src, writes=['stg'], sem='stg')
            for c in range(4):
                S.op('pe', lambda e: e.transpose(out=psum[:, 7, c * 128:c * 128 + nrows],
                                                 in_=stg[0:nrows, c * 128:(c + 1) * 128],
                                                 identity=ident[0:nrows, 0:nrows]),
                     reads=['stg', 'ident'], writes=['ps7'], sig=(c == 3))
            for c in range(4):
                S.op('dve', lambda e: e.tensor_copy(out=dst_fn(c), in_=psum[:, 7, c * 128:c * 128 + nrows]),
                     reads=['ps7'], writes=[key])

        sp_ = AR.alloc([128, 48])
        Rr = AR.alloc([128, 16])
        th = AR.alloc([128, 16])
        BT = AR.alloc([128, 4, 4, 2, 128], BF16)
        CTp = AR.alloc([128, 16, 2, 128], BF16)
        cosT = AR.alloc([128, 16, 128])
        sinT = AR.alloc([128, 16, 128])
        Xp = AR.alloc([128, 16, 2])
        Xs = AR.alloc([128, 16, 2, 16])
        Abr = AR.alloc([128, 16])
        Abi = AR.alloc([128, 16])
        ssm_off = AR.off

        def sincos(out_sin, out_cos, ang, shape, tmpa, tmpi):
            for (dst, shift) in ((out_sin, 0.0), (out_cos, PI / 2)):
                S.op('dve', lambda e: e.tensor_scalar(out=tmpa, in0=ang, scalar1=1.0 / (2 * PI),
                                                      scalar2=shift / (2 * PI), op0=ALU.mult, op1=ALU.add),
                     reads=['sang'], writes=['stmpa'])
                S.op('dve', lambda e: e.tensor_copy(out=tmpi, in_=tmpa), reads=['stmpa'], writes=['stmpi'])
                S.op('dve', lambda e: e.tensor_copy(out=tmpa, in_=tmpi), reads=['stmpi'], writes=['stmpa'])
                S.op('dve', lambda e: e.scalar_tensor_tensor(out=tmpa, in0=tmpa, scalar=-2 * PI, in1=ang,
                                                             op0=ALU.mult, op1=ALU.add),
                     reads=['stmpa', 'sang'], writes=['stmpa'])
                S.op('dve', lambda e: e.tensor_scalar(out=dst, in0=tmpa, scalar1=shift, scalar2=0.0, op0=ALU.add,
                                                      op1=ALU.add), reads=['stmpa'], writes=['sdst'])
                S.op('dve', lambda e: e.tensor_scalar(out=tmpa, in0=dst, scalar1=PI, scalar2=-2 * PI, op0=ALU.is_gt,
                                                      op1=ALU.mult), reads=['sdst'], writes=['stmpa'])
                S.op('dve', lambda e: e.tensor_tensor(out=dst, in0=dst, in1=tmpa, op=ALU.add),
                     reads=['sdst', 'stmpa'], writes=['sdst'])
                S.op('dve', lambda e: e.tensor_scalar(out=tmpa, in0=dst, scalar1=-PI, scalar2=2 * PI, op0=ALU.is_lt,
                                                      op1=ALU.mult), reads=['sdst'], writes=['stmpa'])
                S.op('dve', lambda e: e.tensor_tensor(out=dst, in0=dst, in1=tmpa, op=ALU.add),
                     reads=['sdst', 'stmpa'], writes=['sdst'])
                S.op('act', lambda e: e.activation(out=dst, in_=dst, func=AF.Sin), reads=['sdst'], writes=['sdst'])

        def ssm_setup():
            m0 = AR.off
            rows = AR.alloc([128, 128])
            ldt = AR.alloc([128, 2])
            S.op('dve', lambda e: e.memset(rows, 0.0), writes=['srows'])
            S.dma('sp', rows[0:16, :], W['ssm_a_re'][l].rearrange("(q g) p -> q (g p)", g=2), writes=['srows'],
                  sem='srows')
            S.dma('sp', rows[16:32, :], W['ssm_a_im'][l].rearrange("(q g) p -> q (g p)", g=2), writes=['srows'],
                  sem='srows')
            S.dma('sp', ldt[32:48, :], W['ssm_log_dt'][l].rearrange("(q g) -> q g", g=2), writes=['sldt'], sem='sldt')
            S.op('dve', lambda e: e.tensor_copy(out=rows[32:48, :].rearrange("p (g x) -> p g x", x=64),
                                                in_=ldt[32:48, :].unsqueeze(2).to_broadcast([16, 2, 64])),
                 reads=['sldt', 'srows'], writes=['srows'])
            S.op('pe', lambda e: e.transpose(out=psum[:, 6, 0:48], in_=rows[0:48, :], identity=ident[0:48, 0:48]),
                 reads=['srows', 'ident'], writes=['ps6'])
            S.op('dve', lambda e: e.tensor_copy(out=sp_, in_=psum[:, 6, 0:48]), reads=['ps6'], writes=['ssp'])
            ar, ai, dt = sp_[:, 0:16], sp_[:, 16:32], sp_[:, 32:48]
            S.op('act', lambda e: e.activation(out=dt, in_=dt, func=AF.Exp), reads=['ssp'], writes=['ssp'])
            S.op('dve', lambda e: e.tensor_tensor(out=Rr, in0=ar, in1=dt, op=ALU.mult), reads=['ssp'], writes=['sR'])
            S.op('act', lambda e: e.activation(out=Rr, in_=Rr, func=AF.Exp), reads=['sR'], writes=['sR'])
            S.op('dve', lambda e: e.tensor_tensor(out=th, in0=ai, in1=dt, op=ALU.mult), reads=['ssp'],
                 writes=['sang'])
            sn = AR.alloc([128, 16])
            cs = AR.alloc([128, 16])
            ta = AR.alloc([128, 16])
            ti = AR.alloc([128, 16]).bitcast(mybir.dt.int32)
            sincos(sn, cs, th, None, ta, ti)
            S.barrier()
            S.op('dve', lambda e: e.tensor_tensor(out=Abr, in0=Rr, in1=cs, op=ALU.mult), writes=['sAb'])
            S.op('dve', lambda e: e.tensor_tensor(out=Abi, in0=Rr, in1=sn, op=ALU.mult), writes=['sAb'])
            S.barrier()
            den = AR.alloc([128, 16])
            nr = AR.alloc([128, 16])
            t1 = AR.alloc([128, 16])
            t2 = AR.alloc([128, 16])
            cr = AR.alloc([128, 16])
            ci = AR.alloc([128, 16])

            def dv(fn):
                S.op('dve', fn, reads=['sx'], writes=['sx'])
            dv(lambda e: e.tensor_tensor(out=den, in0=ar, in1=ar, op=ALU.mult))
            dv(lambda e: e.tensor_tensor(out=t1, in0=ai, in1=ai, op=ALU.mult))
            dv(lambda e: e.tensor_tensor(out=den, in0=den, in1=t1, op=ALU.add))
            dv(lambda e: e.reciprocal(out=den, in_=den))
            dv(lambda e: e.tensor_scalar(out=nr, in0=Abr, scalar1=-1.0, scalar2=0.0, op0=ALU.add, op1=ALU.add))
            dv(lambda e: e.tensor_tensor(out=t1, in0=nr, in1=ar, op=ALU.mult))
            dv(lambda e: e.tensor_tensor(out=t2, in0=Abi, in1=ai, op=ALU.mult))
            dv(lambda e: e.tensor_tensor(out=t1, in0=t1, in1=t2, op=ALU.add))
            dv(lambda e: e.tensor_tensor(out=cr, in0=t1, in1=den, op=ALU.mult))
            dv(lambda e: e.tensor_tensor(out=t1, in0=Abi, in1=ar, op=ALU.mult))
            dv(lambda e: e.tensor_tensor(out=t2, in0=nr, in1=ai, op=ALU.mult))
            dv(lambda e: e.tensor_tensor(out=t1, in0=t1, in1=t2, op=ALU.subtract))
            dv(lambda e: e.tensor_tensor(out=ci, in0=t1, in1=den, op=ALU.mult))
            br = AR.alloc([128, 16, 16])
            bi = AR.alloc([128, 16, 16])
            S.dma('sp', br, W['ssm_b_re'][l].rearrange("(q g) p h -> (g p) q h", g=2), writes=['sbr'], sem='sbr')
            S.dma('sp', bi, W['ssm_b_im'][l].rearrange("(q g) p h -> (g p) q h", g=2), writes=['sbi'], sem='sbi')
            S.barrier()
            Bbr = AR.alloc([128, 16, 16])
            Bbi = AR.alloc([128, 16, 16])
            tA = AR.alloc([128, 16, 16])
            crb = cr.unsqueeze(2).to_broadcast([128, 16, 16])
            cib = ci.unsqueeze(2).to_broadcast([128, 16, 16])
            dv(lambda e: e.tensor_tensor(out=Bbr, in0=br, in1=crb, op=ALU.mult))
            dv(lambda e: e.tensor_tensor(out=tA, in0=bi, in1=cib, op=ALU.mult))
            dv(lambda e: e.tensor_tensor(out=Bbr, in0=Bbr, in1=tA, op=ALU.subtract))
            dv(lambda e: e.tensor_tensor(out=Bbi, in0=bi, in1=crb, op=ALU.mult))
            dv(lambda e: e.tensor_tensor(out=tA, in0=br, in1=cib, op=ALU.mult))
            dv(lambda e: e.tensor_tensor(out=Bbi, in0=Bbi, in1=tA, op=ALU.add))
            Zb = AR.alloc([128, 4 * 2 * 128])
            dv(lambda e: e.memset(Zb, 0.0))
            Zv = Zb.rearrange("p (j r q g h) -> p j r q g h", j=4, r=2, q=4, g=2, h=16)
            for ri, Bb in enumerate((Bbr, Bbi)):
                Bv = Bb.rearrange("p (j q) h -> p j q h", q=4)
                dv(lambda e: e.tensor_copy(out=Zv[0:64, :, ri, :, 0, :], in_=Bv[0:64]))
                dv(lambda e: e.tensor_copy(out=Zv[64:128, :, ri, :, 1, :], in_=Bv[64:128]))
            Z4 = Zb.rearrange("p (j r x) -> p j r x", j=4, r=2)
            S.barrier()
            for j in range(4):
                for ri in range(2):
                    S.op('pe', lambda e: e.transpose(out=psum[:, 6, 0:128], in_=Z4[:, j, ri, :], identity=ident[:]),
                         reads=['ident'], writes=['ps6'])
                    for qq in range(4):
                        S.op('dve', lambda e: e.tensor_scalar(out=BT[:, j, qq, ri, :], in0=psum[:, 6, 0:128],
                                                              scalar1=rmk[:, qq:qq + 1], scalar2=0.0, op0=ALU.mult, op1=ALU.add),
                             reads=['ps6', 'rmk'], writes=['sBT'])
            Cn = AR.alloc([128, 2, 4, 64])
            S.dma('sp', Cn[:, 0], W['ssm_c_re'][l].rearrange("(j g) h p -> (g h) j p", j=4), writes=['sCn'],
                  sem='sCn')
            S.dma('sp', Cn[:, 1], W['ssm_c_im'][l].rearrange("(j g) h p -> (g h) j p", j=4), writes=['sCn'],
                  sem='sCn')
            S.barrier()
            Cm = AR.alloc([128, 2, 4, 2, 64])
            for ri in range(2):
                for j in range(4):
                    dv(lambda e: e.tensor_tensor(out=Cm[:, ri, j], in0=Cn[:, ri, j].unsqueeze(1).to_broadcast([128, 2, 64]),
                                                 in1=g2m[:, :].unsqueeze(2).to_broadcast([128, 2, 64]), op=ALU.mult))
            dv(lambda e: e.memset(CTp, 0.0))
            S.barrier()
            for ri in range(2):
                for j in range(4):
                    S.op('pe', lambda e: e.transpose(out=psum[:, 6, 0:128],
                                                     in_=Cm[:, ri, j].rearrange("p g x -> p (g x)"), identity=ident[:]),
                         reads=['ident'], writes=['ps6'])
                    for qq in range(4):
                        if ri == 0:
                            S.op('dve', lambda e: e.tensor_copy(out=CTp[:, 4 * j + qq, ri, 32 * qq:32 * qq + 32],
                                                                in_=psum[:, 6, 32 * qq:32 * qq + 32]),
                                 reads=['ps6'], writes=['sCT'])
                        else:
                            S.op('dve', lambda e: e.tensor_scalar(out=CTp[:, 4 * j + qq, ri, 32 * qq:32 * qq + 32],
                                                                  in0=psum[:, 6, 32 * qq:32 * qq + 32], scalar1=-1.0,
                                                                  scalar2=0.0, op0=ALU.mult, op1=ALU.add),
                                 reads=['ps6'], writes=['sCT'])
            S.barrier()
            AR.off = m0
            ang = cosT
            ta3 = AR.alloc([128, 16, 128])
            ti3 = AR.alloc([128, 16, 128]).bitcast(mybir.dt.int32)
            S.op('dve', lambda e: e.tensor_tensor(out=ang, in0=th.unsqueeze(2).to_broadcast([128, 16, 128]),
                                                  in1=tv[:, :].unsqueeze(1).to_broadcast([128, 16, 128]), op=ALU.mult),
                 writes=['sang'])
            sincos(sinT, cosT, ang, None, ta3, ti3)
            S.barrier()
            AR.off = m0

        def ssm_prompt_pass(full):
            m0 = AR.off
            Sp = AR.alloc([128, 4, 2, 128])
            Vv = AR.alloc([128, 4, 2, 128])
            Tt = AR.alloc([128, 4, 2, 128])
            Xb = AR.alloc([128, 4, 2, 128], BF16)
            yacc = AR.alloc([128, 4, 128]) if full else None
            for b in range(8):
                c0 = b * 128
                for j in range(4):
                    pb = 4 + 2 * (j % 2)
                    for qq in range(4):
                        for ri in range(2):
                            S.op('pe', lambda e: e.matmul(psum[:, pb + qq // 2, ((qq % 2) * 2 + ri) * 128:((qq % 2) * 2 + ri + 1) * 128],
                                                          lhsT=BT[:, j, qq, ri, :],
                                                          rhs=u_bf[:, j, c0:c0 + 128],
                                                          start=True, stop=True),
                                 reads=['sBT', 'u_bf'], writes=['ps%d' % (pb + qq // 2)], sig=(ri == 1))
                    bu = psum[:, pb:pb + 2, :].rearrange("p a (q r t) -> p (a q) r t", q=2, r=2)
                    cb_ = cosT[:, 4 * j:4 * j + 4, :]
                    sb_ = sinT[:, 4 * j:4 * j + 4, :]
                    rk = ['ps%d' % pb, 'ps%d' % (pb + 1)]
                    S.op('dve', lambda e: e.tensor_tensor(out=Sp[:, :, 0, :], in0=bu[:, :, 0, :], in1=cb_, op=ALU.mult), reads=rk, writes=['sSp0'])
                    S.op('dve', lambda e: e.tensor_tensor(out=Tt[:, :, 0, :], in0=bu[:, :, 1, :], in1=sb_, op=ALU.mult), reads=rk, writes=['sTt0'])
                    S.op('dve', lambda e: e.tensor_tensor(out=Sp[:, :, 1, :], in0=bu[:, :, 1, :], in1=cb_, op=ALU.mult), reads=rk, writes=['sSp1'])
                    S.op('dve', lambda e: e.tensor_tensor(out=Tt[:, :, 1, :], in0=bu[:, :, 0, :], in1=sb_, op=ALU.mult), reads=rk, writes=['sTt1'])
                    S.op('dve', lambda e: e.tensor_tensor(out=Sp[:, :, 0, :], in0=Sp[:, :, 0, :], in1=Tt[:, :, 0, :], op=ALU.add), reads=['sSp0', 'sTt0'], writes=['sSp0'])
                    S.op('dve', lambda e: e.tensor_tensor(out=Sp[:, :, 1, :], in0=Sp[:, :, 1, :], in1=Tt[:, :, 1, :], op=ALU.subtract), reads=['sSp1', 'sTt1'], writes=['sSp1'])
                    for qq in range(4):
                        q = 4 * j + qq
                        for ri in range(2):
                            S.op('dve', lambda e: e.tensor_tensor_scan(out=Vv[:, qq, ri, :],
                                                                       data0=Rr[:, q:q + 1].to_broadcast([128, 128]),
                                                                       data1=Sp[:, qq, ri, :], initial=Xp[:, q, ri:ri + 1],
                                                                       op0=ALU.mult, op1=ALU.add),
                                 reads=['sSp%d' % ri, 'sXp'], writes=['sVv%d' % ri])
                    cs = slice(0, 128) if full else slice(127, 128)
                    pe2 = 'dve'
                    S.op('dve', lambda e: e.tensor_tensor(out=Sp[:, :, 0, cs], in0=Vv[:, :, 0, cs], in1=cb_[:, :, cs], op=ALU.mult), reads=['sVv0'], writes=['sSp0'])
                    S.op(pe2, lambda e: e.tensor_tensor(out=Tt[:, :, 0, cs], in0=Vv[:, :, 1, cs], in1=sb_[:, :, cs], op=ALU.mult), reads=['sVv1'], writes=['sTt0'])
                    S.op('dve', lambda e: e.tensor_tensor(out=Sp[:, :, 1, cs], in0=Vv[:, :, 0, cs], in1=sb_[:, :, cs], op=ALU.mult), reads=['sVv0'], writes=['sSp1'])
                    S.op(pe2, lambda e: e.tensor_tensor(out=Tt[:, :, 1, cs], in0=Vv[:, :, 1, cs], in1=cb_[:, :, cs], op=ALU.mult), reads=['sVv1'], writes=['sTt1'])
                    S.op('dve', lambda e: e.tensor_tensor(out=Sp[:, :, 0, cs], in0=Sp[:, :, 0, cs], in1=Tt[:, :, 0, cs], op=ALU.subtract), reads=['sSp0', 'sTt0'], writes=['sSp0'])
                    S.op('dve', lambda e: e.tensor_tensor(out=Sp[:, :, 1, cs], in0=Sp[:, :, 1, cs], in1=Tt[:, :, 1, cs], op=ALU.add), reads=['sSp1', 'sTt1'], writes=['sSp1'])
                    S.op('dve', lambda e: e.tensor_copy(out=Xp[:, 4 * j:4 * j + 4, :], in_=Sp[:, :, :, 127]), reads=['sSp0', 'sSp1'], writes=['sXp'])
                    if full:
                        S.op('act', lambda e: e.activation(out=Xb, in_=Sp, func=AF.Copy), reads=['sSp0', 'sSp1'], writes=['sXb'])
                        for qq in range(4):
                            for ri in range(2):
                                S.op('pe', lambda e: e.matmul(psum[:, 3, j * 128:(j + 1) * 128], lhsT=CTp[:, 4 * j + qq, ri, :],
                                                              rhs=Xb[:, qq, ri, :], start=(qq == 0 and ri == 0),
                                                              stop=(qq == 3 and ri == 1)),
                                     reads=['sCT', 'sXb'], writes=['ps3'], sig=(qq == 3 and ri == 1))
                if full:
                    for j in range(4):
                        S.op('dve', lambda e: e.scalar_tensor_tensor(out=yacc[:, j, :], in0=u_bf[:, j, c0:c0 + 128],
                                                                     scalar=pv(l, 'ssm_d', j), in1=psum[:, 3, j * 128:(j + 1) * 128],
                                                                     op0=ALU.mult, op1=ALU.add),
                             reads=['ps3', 'u_bf', 'pvec'], writes=['syacc'])
                    S.op('act', lambda e: e.activation(out=om[:, :, c0:c0 + 128], in_=yacc, func=AF.Gelu),
                         reads=['syacc'], writes=['om'])
            S.barrier()
            AR.off = m0

        def ssm_sample(full):
            m0 = AR.off
            bus = AR.alloc([128, 4, 2, 128])
            t1 = AR.alloc([128, 4, 2, 16])
            t2 = AR.alloc([128, 4, 2, 16])
            Xall = AR.alloc([128, 4, 2, 128])
            Xb = AR.alloc([128, 4, 2, 128], BF16)
            yacc = AR.alloc([128, 4, 128])
            for j in range(4):
                pb = 4 + 2 * (j % 2)
                for qq in range(4):
                    for ri in range(2):
                        S.op('pe', lambda e: e.matmul(psum[:, pb + qq // 2, ((qq % 2) * 2 + ri) * 128:((qq % 2) * 2 + ri + 1) * 128],
                                                      lhsT=BT[:, j, qq, ri, :],
                                                      rhs=u_bf[:, j, NP:NT], start=True, stop=True),
                             reads=['sBT', 'u_bf'], writes=['ps%d' % (pb + qq // 2)], sig=(ri == 1))
                S.op('dve', lambda e: e.tensor_copy(out=bus, in_=psum[:, pb:pb + 2, :].rearrange("p a (q r t) -> p (a q) r t", q=2, r=2)),
                     reads=['ps%d' % pb, 'ps%d' % (pb + 1)], writes=['sbus'])
                busv = bus.rearrange("p q r (n l) -> p q r n l", l=8)
                Xv = Xall.rearrange("p q r (n l) -> p q r n l", l=8)
                arb = Abr[:, 4 * j:4 * j + 4].unsqueeze(2).unsqueeze(3).to_broadcast([128, 4, 2, 16])
                aib = Abi[:, 4 * j:4 * j + 4].unsqueeze(2).to_broadcast([128, 4, 16])
                for st in range(8):
                    prev = Xs[:, 4 * j:4 * j + 4] if st == 0 else Xv[:, :, :, :, st - 1]
                    kprev = ['sXs'] if st == 0 else ['sXall']
                    S.op('dve', lambda e: e.tensor_tensor(out=t1, in0=prev, in1=arb, op=ALU.mult), reads=kprev, writes=['st1'])
                    S.op('dve', lambda e: e.tensor_tensor(out=t2[:, :, 0, :], in0=prev[:, :, 1, :], in1=aib, op=ALU.mult), reads=kprev, writes=['st2'])
                    S.op('dve', lambda e: e.tensor_tensor(out=t2[:, :, 1, :], in0=prev[:, :, 0, :], in1=aib, op=ALU.mult), reads=kprev, writes=['st2'])
                    S.op('dve', lambda e: e.tensor_tensor(out=t1[:, :, 0, :], in0=t1[:, :, 0, :], in1=t2[:, :, 0, :], op=ALU.subtract), reads=['st1', 'st2'], writes=['st1'])
                    S.op('dve', lambda e: e.tensor_tensor(out=t1[:, :, 1, :], in0=t1[:, :, 1, :], in1=t2[:, :, 1, :], op=ALU.add), reads=['st1', 'st2'], writes=['st1'])
                    S.op('dve', lambda e: e.tensor_tensor(out=Xv[:, :, :, :, st], in0=t1, in1=busv[:, :, :, :, st], op=ALU.add), reads=['st1', 'sbus'], writes=['sXall'])
                S.op('dve', lambda e: e.tensor_copy(out=Xs[:, 4 * j:4 * j + 4], in_=Xv[:, :, :, :, 7]), reads=['sXall'], writes=['sXs'])
                S.op('dve', lambda e: e.tensor_copy(out=Xb, in_=Xall), reads=['sXall'], writes=['sXb'])
                for qq in range(4):
                    for ri in range(2):
                        S.op('pe', lambda e: e.matmul(psum[:, 3, j * 128:(j + 1) * 128], lhsT=CTp[:, 4 * j + qq, ri, :],
                                                      rhs=Xb[:, qq, ri, :], start=(qq == 0 and ri == 0), stop=(qq == 3 and ri == 1)),
                             reads=['sCT', 'sXb'], writes=['ps3'], sig=(qq == 3 and ri == 1))
            for j in range(4):
                S.op('dve', lambda e: e.scalar_tensor_tensor(out=yacc[:, j, :], in0=u_bf[:, j, NP:NT], scalar=pv(l, 'ssm_d', j),
                                                             in1=psum[:, 3, j * 128:(j + 1) * 128], op0=ALU.mult, op1=ALU.add),
                     reads=['ps3', 'u_bf', 'pvec'], writes=['syacc'])
            S.op('act', lambda e: e.activation(out=om[:, :, NP:NT], in_=yacc, func=AF.Gelu), reads=['syacc'], writes=['om'])
            S.barrier()
            AR.off = m0

        ssm_setup()
        if stage < 1:
            return
        load_win(1536)
        for m in range(4):
            base = 0 if m % 2 == 0 else 3
            proj_fm(m, base)
            evac(v3(u_bf[:, m, :]), bank3(base), pskeys(base), ['u_bf'])
        S.op('dve', lambda e: e.memset(Xp, 0.0), writes=['sXp'])
        ssm_prompt_pass(False)
        if stage < 2:
            return
        S.op('dve', lambda e: e.tensor_copy(out=xsb[:, 180:212], in_=Xp.rearrange("p q r -> p (q r)")), reads=['sXp'], writes=['xsb'])
        m0 = AR.off
        lt = AR.alloc([128, 12, 128])
        for blk, col0 in enumerate((0, 2048, 2560)):
            load_win(col0)
            for m in range(4):
                for k in range(16):
                    S.op('pe', lambda e: e.matmul(psum[:, 6, m * 128:(m + 1) * 128], lhsT=wins[:, k, m * 128:(m + 1) * 128],
                                                  rhs=xn[:, k, NP - 128:NP], start=(k == 0), stop=(k == 15)),
                         reads=['win', XK[k]], writes=['ps6'], sig=(k == 15))
            if blk < 2:
                evac(lt[:, blk * 4:blk * 4 + 4, :], psum[:, 6, :].rearrange("p (a b) -> p a b", b=128), ['ps6'], ['lt'])
            else:
                S.op('act', lambda e: e.activation(out=lt[:, 8:12, :], in_=psum[:, 6, :].rearrange("p (a b) -> p a b", b=128),
                                                   func=AF.Sigmoid), reads=['ps6'], writes=['lt'])
        S.op('dve', lambda e: e.tensor_copy(out=xsb[:, 0:60].rearrange("p (c t) -> p c t", t=15), in_=lt[:, 0:4, 113:128]),
             reads=['lt'], writes=['xsb'])
        S.op('dve', lambda e: e.tensor_tensor(out=xsb[:, 60:180].rearrange("p (c t) -> p c t", t=30), in0=lt[:, 4:8, 98:128],
                                              in1=lt[:, 8:12, 98:128], op=ALU.mult), reads=['lt'], writes=['xsb'])
        AR.off = m0
        if stage < 3:
            return
        xi, xo = XB[l]
        S.barrier()
        S.dma('pool', xi.ap()[:, :], xsb, reads=['xsb'], writes=['xb_dram'], sem='xb')
        import os
        sub = os.environ.get('XSUB', 'z')
        if sub == 'a':
            S.barrier()
            return
        S._deps('pool', ['xb_dram'], ['xbo_dram'])
        cch = S.dsem('cc')
        nc.gpsimd.collective_compute("AllGather", ALU.bypass, replica_groups=[[0, 1], [2, 3], [4, 5], [6, 7]],
                                     ins=[xi.ap().opt()], outs=[xo.ap().opt()]).then_inc(cch)
        S.cnt['cc'] += 1
        S._record(('cc', S.cnt['cc']), ['xb_dram'], ['xbo_dram'])
        if sub == 'b':
            S.barrier()
            return
        cc_tag = ('cc', S.cnt['cc'])

        def exchange_readback():
            qq_ = 'pool' if os.environ.get('XVAR', '') == 'B' else 'sp'
            S._wait(qq_, cc_tag)
            S.dma(qq_, xrv, xo.ap()[0:128, :], writes=['xrv'], sem='xrv')
        if os.environ.get('XVAR', '') == 'A':
            S.barrier()
        if stage < 4:
            exchange_readback()
            S.barrier()
            return

        m0 = AR.off
        vtok = AR.alloc([128, 9, 512], BF16)
        vs32 = AR.alloc([128, 512])
        wsg = AR.alloc([128, 4, 128])
        wTm = AR.alloc([128, 4, 128], BF16)
        wblk = AR.alloc([128, 4, 128], BF16)
        x8 = AR.alloc([128, 128], BF16)
        e8b = AR.alloc([128, 128], BF16)
        w8b = AR.alloc([128, 4, 8], BF16)
        sgb = AR.alloc([128, 4, 128], BF16)
        sgbs = AR.alloc([128, 4, 128], BF16)
        mixT = AR.alloc([128, NT])
        S.dma('sp', wsg, W['sg_w'][l].rearrange("h t s -> t h s"), writes=['wsg'], sem='wsg')
        S.op('dve', lambda e: e.tensor_copy(out=e8b[0:8, :], in_=e8[:, :]), reads=['e8'], writes=['e8b'])
        S.dma('pool', sgb[0:1], W['sg_b'][l].rearrange("(o h) t -> o h t", o=1), writes=['sgb'], sem='sgb')
        S.dma('pool', sgbs[0:1].rearrange("o h (n t) -> o h n t", t=8),
              W['sg_b'][l][:, 0:8].rearrange("(o h) t -> o h t", o=1).unsqueeze(2).to_broadcast([1, 4, 16, 8]),
              writes=['sgbs'], sem='sgbs')
        for hd in range(4):
            S.op('pe', lambda e: e.transpose(out=psum[:, 6, 0:128], in_=wsg[:, hd, :], identity=ident[:]),
                 reads=['wsg', 'ident'], writes=['ps6'])
            S.op('dve', lambda e: e.tensor_tensor(out=wTm[:, hd, :], in0=psum[:, 6, 0:128], in1=tri[:], op=ALU.mult),
                 reads=['ps6', 'tri'], writes=['wTm'])
            S.op('dve', lambda e: e.tensor_copy(out=w8b[0:8, hd, :], in_=wsg[0:8, hd, 0:8]), reads=['wsg'], writes=['w8b'])
            S.op('pe', lambda e: e.matmul(psum[0:8, 7, 0:128], lhsT=w8b[0:8, hd, :], rhs=e8b[0:8, :], start=True, stop=True),
                 reads=['w8b', 'e8b'], writes=['ps7'])
            S.op('dve', lambda e: e.tensor_copy(out=x8[0:8, :], in_=psum[0:8, 7, 0:128]), reads=['ps7'], writes=['x8'])
            S.op('pe', lambda e: e.matmul(psum[:, 7, 128:256], lhsT=e8b[0:8, :], rhs=x8[0:8, :], start=True, stop=True),
                 reads=['x8', 'e8b'], writes=['ps7'])
            S.op('dve', lambda e: e.tensor_tensor(out=wblk[:, hd, :], in0=psum[:, 7, 128:256], in1=mblk[:], op=ALU.mult),
                 reads=['ps7', 'mblk'], writes=['wblk'])
        if sub == 's1':
            S.barrier()
            return
        load_win(1024)
        for i in range(9 if sub != 's2a' else 1):
            bk = 4 + (i % 2)
            for k in range(16):
                S.op('pe', lambda e: e.matmul(psum[:, bk, :], lhsT=xn[:, k, i * 128:(i + 1) * 128], rhs=wins[:, k, :],
                                              start=(k == 0), stop=(k == 15)),
                     reads=['win', XK[k]], writes=['ps%d' % bk], sig=(k == 15))
            if i < 8:
                evac(vtok[:, i, :], psum[:, bk, :], ['ps%d' % bk], ['vtok'])
            else:
                S.op('dve', lambda e: e.tensor_copy(out=vs32, in_=psum[:, bk, :]), reads=['ps%d' % bk], writes=['vs32'])
                S.op('act', lambda e: e.activation(out=vtok[:, i, :], in_=vs32, func=AF.Copy), reads=['vs32'], writes=['vtok'])
                S.dma('sp', o_sgv[l], vs32, reads=['vs32'], sem='vs32')
        if sub in ('s2', 's2a', 's2b', 's2c'):
            S.barrier()
            return
        load_win(512)
        for hd in range(4):
            for i in range(9):
                bk = 4 + i // 4
                co = (i % 4) * 128
                rhs_w = wTm[:, hd, :] if i < 8 else wblk[:, hd, :]
                brow = sgb[0:1, hd, :] if i < 8 else sgbs[0:1, hd, :]
                S.op('pe', lambda e: e.matmul(psum[:, bk, co:co + 128], lhsT=vtok[:, i, hd * 128:(hd + 1) * 128], rhs=rhs_w,
                                              start=True, stop=False), reads=['vtok', 'wTm', 'wblk'], writes=['ps%d' % bk], sig=False)
                S.op('pe', lambda e: e.matmul(psum[:, bk, co:co + 128], lhsT=ones_bf[0:1, :], rhs=brow, start=False, stop=True),
                     reads=['sgb', 'sgbs', 'ones_bf'], writes=['ps%d' % bk], sig=(i % 4 == 3 or i == 8))
            S.op('dve', lambda e: e.tensor_copy(out=mixT[:, 0:1024].rearrange("p (a b) -> p a b", b=512), in_=psum[:, 4:6, :]),
                 reads=['ps4', 'ps5'], writes=['mixT'])
            S.op('dve', lambda e: e.tensor_copy(out=mixT[:, 1024:NT], in_=psum[:, 6, 0:128]), reads=['ps6'], writes=['mixT'])
            proj_fm(hd, 0)
            S.op('dve', lambda e: e.tensor_tensor(out=v3(om[:, hd, :]), in0=bank3(0), in1=v3(mixT), op=ALU.mult),
                 reads=pskeys(0) + ['mixT'], writes=['om'])
        S.barrier()
        AR.off = m0
        if sub == 's3':
            return
        finish(1)
        exchange_readback()
        S.barrier()
        if stage < 5:
            return

        S.op('dve', lambda e: e.tensor_tensor(out=Xp.rearrange("p q r -> p (q r)"), in0=xrv[:, 180:212], in1=modd[:, 0:1].to_broadcast([128, 32]), op=ALU.mult), reads=['xrv', 'modd'], writes=['sXp'])
        m0 = AR.off
        sst = AR.alloc([128, 2, 2048])
        S.dma('sp', sst[0:16, 0, :], st_re[l], writes=['sst'], sem='sst')
        S.dma('sp', sst[0:16, 1, :], st_im[l], writes=['sst'], sem='sst')
        for ri in range(2):
            for q in range(16):
                S.op('pe', lambda e: e.transpose(out=psum[:, 6, q * 16:(q + 1) * 16], in_=sst[0:16, ri, q * 128:(q + 1) * 128],
                                                 identity=ident[0:16, 0:16]), reads=['sst', 'ident'], writes=['ps6'], sig=(q == 15))
            S.op('dve', lambda e: e.tensor_copy(out=Xs[:, :, ri, :], in_=psum[:, 6, 0:256].rearrange("p (q n) -> p q n", n=16)),
                 reads=['ps6'], writes=['sXs'])
        S.barrier()
        AR.off = m0
        ssm_prompt_pass(True)
        ssm_sample(True)
        if stage < 6:
            return
        m0 = AR.off
        sso = AR.alloc([128, 2, 2048])
        xpt = AR.alloc([128, 2, 16])
        xps = AR.alloc([128, 2, 128])
        S.op('dve', lambda e: e.tensor_copy(out=xpt, in_=Xp.rearrange("p q r -> p r q")), reads=['sXp'], writes=['xpt'])
        for ri in range(2):
            for q in range(16):
                S.op('pe', lambda e: e.transpose(out=psum[0:16, 4 + q // 4, (q % 4) * 128:(q % 4 + 1) * 128], in_=Xs[:, q, ri, :],
                                                 identity=ident[:]), reads=['sXs', 'ident'], writes=['ps%d' % (4 + q // 4)], sig=(q % 4 == 3))
            S.op('dve', lambda e: e.tensor_copy(out=sso[0:16, ri, :].rearrange("p (a b) -> p a b", b=512), in_=psum[0:16, 4:8, :]),
                 reads=['ps4', 'ps5', 'ps6', 'ps7'], writes=['sso'])
            S.op('pe', lambda e: e.transpose(out=psum[0:16, 3, 0:128], in_=xpt[:, ri, :], identity=ident[:]),
                 reads=['xpt', 'ident'], writes=['ps3'])
            S.op('dve', lambda e: e.tensor_copy(out=xps[0:16, ri, :], in_=psum[0:16, 3, 0:128]), reads=['ps3'], writes=['xps'])
            S.barrier()
        S.dma('sp', o_re[l, 0:16, :], sso[0:16, 0, :], reads=['sso'], sem='sso')
        S.dma('sp', o_im[l, 0:16, :], sso[0:16, 1, :], reads=['sso'], sem='sso')
        S.dma('sp', o_re[l, 16, :].rearrange("(q x) -> q x", x=128), xps[0:16, 0, :], reads=['xps'], sem='sso')
        S.dma('sp', o_im[l, 16, :].rearrange("(q x) -> q x", x=128), xps[0:16, 1, :], reads=['xps'], sem='sso')
        S.barrier()
        AR.off = m0
        m0 = AR.off
        wgl = AR.alloc([128, 4, 512], BF16)
        gsg = AR.alloc([128, NT])
        S.dma('pool', wgl, W['ssm_w_glu'][l].rearrange("(k p) d -> p k d", p=128), writes=['wgl'], sem='wgl')
        gl = AR.alloc([128, 4, NT], BF16)
        for m in range(4):
            for k in range(4):
                for tb, (a, b) in enumerate(TB):
                    S.op('pe', lambda e: e.matmul(psum[:, tb, 0:384], lhsT=wgl[:, k, m * 128:(m + 1) * 128], rhs=om[:, k, a:b],
                                                  start=(k == 0), stop=(k == 3)), reads=['wgl', 'om'], writes=['ps%d' % tb],
                         sig=(k == 3 and tb == 2))
            S.op('act', lambda e: e.activation(out=v3(gsg), in_=bank3(0), func=AF.Sigmoid, bias=pv(l, 'ssm_b_glu', m)),
                 reads=pskeys(0) + ['pvec'], writes=['gsg'])
            S.op('dve', lambda e: e.tensor_tensor(out=gl[:, m, :], in0=om[:, m, :], in1=gsg, op=ALU.mult),
                 reads=['om', 'gsg'], writes=['gl'])
        S.barrier()
        S.op('dve', lambda e: e.tensor_copy(out=om, in_=gl), reads=['gl'], writes=['om'])
        S.barrier()
        AR.off = m0
        finish(2)
        if stage < 7:
            return

        AR.off = base_off
        m0 = AR.off
        pxp = AR.alloc([128, 4, 15 + NP])
        pxs = AR.alloc([128, 4, 16, 23])
        T1 = AR.alloc([128, 15 + NP])
        T2 = AR.alloc([128, 15 + NP])
        U1 = AR.alloc([128, 16, 23])
        U2 = AR.alloc([128, 16, 23])
        zb = AR.alloc([128, 4, NT], BF16)
        rc = AR.alloc([128, 4, 16])
        pw = AR.alloc([128, 4, 128], BF16)
        pstg = AR.alloc([128, 512])
        pcst = AR.alloc([128, 4, 128])
        S.dma('pool', pw, W['pool_w'][l].rearrange("g c d -> c g d"), writes=['pw'], sem='pw')
        for gi, w in enumerate((2, 4, 8, 16)):
            S.op('dve', lambda e: e.tensor_scalar(out=rc[:, gi, :], in0=posb[:, :], scalar1=1.0, scalar2=float(w), op0=ALU.add,
                                                  op1=ALU.min), reads=['posb'], writes=['rc'])
        S.op('dve', lambda e: e.reciprocal(out=rc, in_=rc), reads=['rc'], writes=['rc'])
        S.op('dve', lambda e: e.tensor_tensor(out=pxp[:, :, 0:15], in0=xrv[:, 0:60].rearrange("p (c t) -> p c t", t=15),
                                              in1=modd[:, 0:1].unsqueeze(2).to_broadcast([128, 4, 15]), op=ALU.mult),
             reads=['xrv', 'modd'], writes=['pxp'])
        for hs in range(2):
            tr_in(lambda c: pxs[:, c, 8 * hs:8 * hs + 8, 0:15], 120,
                  st_pool[l, 8 * hs:8 * hs + 8].rearrange("n r c -> (n r) c"), 'pxs', pstg)
        load_win(0)
        for c in range(4):
            base = 0 if c % 2 == 0 else 3
            proj_fm(c, base)
            S.op('act', lambda e: e.activation(out=pxp[:, c, 15:15 + 768].rearrange("p (a b) -> p a b", b=384),
                                               in_=psum[:, base:base + 2, 0:384], func=AF.Copy),
                 reads=pskeys(base, 2), writes=['pxp'])
            S.op('dve', lambda e: e.tensor_copy(out=pxp[:, c, 15 + 768:15 + NP], in_=psum[:, base + 2, 0:256]),
                 reads=['ps%d' % (base + 2)], writes=['pxp'])
            S.op('dve', lambda e: e.tensor_copy(out=pxs[:, c, :, 15:23], in_=psum[:, base + 2, 256:384].rearrange("p (n t) -> p n t", t=8)),
                 reads=['ps%d' % (base + 2)], writes=['pxs'])
        S.barrier()
        for c in range(4):
            w = 2 << c
            srcp, srcs = pxp[:, c, :], pxs[:, c]
            bufp, bufs_ = [T1, T2], [U1, U2]
            sh = 1
            lo = 0
            for lev in range(c + 1):
                dp, ds = bufp[lev % 2], bufs_[lev % 2]
                nlo = lo + sh
                S.op('dve', lambda e: e.tensor_tensor(out=dp[:, nlo:], in0=srcp[:, nlo:], in1=srcp[:, nlo - sh:15 + NP - sh], op=ALU.add),
                     reads=['pxp', 'pT'], writes=['pT'])
                S.op('dve', lambda e: e.tensor_tensor(out=ds[:, :, nlo:], in0=srcs[:, :, nlo:], in1=srcs[:, :, nlo - sh:23 - sh], op=ALU.add),
                     reads=['pxs', 'pT'], writes=['pT'])
                srcp, srcs = dp, ds
                lo = nlo
                sh *= 2
            S.op('dve', lambda e: e.scalar_tensor_tensor(out=zb[:, c, 0:NP], in0=srcp[:, 15:], scalar=1.0 / w, in1=pxp[:, c, 15:],
                                                         op0=ALU.mult, op1=ALU.subtract), reads=['pT', 'pxp'], writes=['zb'])
            S.op('dve', lambda e: e.scalar_tensor_tensor(out=zb[:, c, NP:NT].rearrange("p (n t) -> p n t", t=8), in0=srcs[:, :, 15:],
                                                         scalar=1.0 / w, in1=pxs[:, c, :, 15:], op0=ALU.mult, op1=ALU.subtract),
                 reads=['pT', 'pxs'], writes=['zb'])
            S.op('dve', lambda e: e.tensor_tensor(out=U1[:, 0, 0:16], in0=srcp[:, 15:31], in1=rc[:, c, :], op=ALU.mult),
                 reads=['pT', 'rc'], writes=['pT2'])
            S.op('dve', lambda e: e.tensor_tensor(out=zb[:, c, 0:16], in0=U1[:, 0, 0:16], in1=pxp[:, c, 15:31], op=ALU.subtract),
                 reads=['pT2', 'pxp', 'zb'], writes=['zb'])
            S.barrier()
        for c in range(4):
            for tb, (a, b) in enumerate(TB):
                S.op('pe', lambda e: e.matmul(psum[:, tb, 0:384], lhsT=pw[:, c, :], rhs=zb[:, c, a:b], start=True, stop=True),
                     reads=['pw', 'zb'], writes=['ps%d' % tb], sig=(tb == 2))
            S.op('dve', lambda e: e.tensor_scalar(out=v3(om[:, c, :]), in0=bank3(0), scalar1=pv(l, 'pool_scale', c), scalar2=0.0,
                                                  op0=ALU.mult, op1=ALU.add), reads=pskeys(0) + ['pvec'], writes=['om'])
        tr_out(lambda c: pxp[:, c, NP:NP + 15], 15, o_pool[l, 16], 'pxp', pstg, pcst)
        for hs in range(2):
            tr_out(lambda c: pxs[:, c, 8 * hs:8 * hs + 8, 8:23], 120, o_pool[l, 8 * hs:8 * hs + 8].rearrange("n r c -> (n r) c"), 'pxs', pstg, pcst)
        S.barrier()
        AR.off = m0
        finish(0)
        if stage < 8:
            return

        m0 = AR.off
        reg = AR.alloc([128, 4216])
        hxb = reg[:, 0:2108].bitcast(BF16).rearrange("p (c t) -> p c t", t=30 + NP)
        hlast = reg[:, 2108:2228].rearrange("p (c t) -> p c t", t=30)
        diag = reg[:, 2228:2228 + 1984].bitcast(BF16).rearrange("p (j d) -> p j d", d=128)
        hxs = AR.alloc([128, 4, 16, 38])
        yv = AR.alloc([128, 4, NT])
        sgt = AR.alloc([128, NT])
        cw = AR.alloc([128, 4, 31])
        ccst = AR.alloc([128, 4, 128])
        S.op('dve', lambda e: e.tensor_tensor(out=hxb[:, :, 0:30], in0=xrv[:, 60:180].rearrange("p (c t) -> p c t", t=30),
                                              in1=modd[:, 0:1].unsqueeze(2).to_broadcast([128, 4, 30]), op=ALU.mult),
             reads=['xrv', 'modd'], writes=['hxb'])
        for hs in range(4):
            tr_in(lambda c: hxs[:, c, 4 * hs:4 * hs + 4, 0:30], 120,
                  st_conv[l, 4 * hs:4 * hs + 4].rearrange("n r c -> (n r) c"), 'hxs', sgt[:, 512:1024])
        cwr = sgt[:, 0:512]
        S.dma('sp', cwr[0:31, :], W['conv_w'][l], writes=['cwr'], sem='cwr')
        for c in range(4):
            S.op('pe', lambda e: e.transpose(out=psum[:, 6, c * 32:c * 32 + 31], in_=cwr[0:31, c * 128:(c + 1) * 128],
                                             identity=ident[0:31, 0:31]), reads=['cwr', 'ident'], writes=['ps6'], sig=(c == 3))
        S.op('dve', lambda e: e.tensor_copy(out=cw, in_=psum[:, 6, 0:128].rearrange("p (c j) -> p c j", j=32)[:, :, 0:31]),
             reads=['ps6'], writes=['cw'])
        S.barrier()
        load_win(2560)
        for c in range(4):
            proj_fm(c, 0)
            S.op('act', lambda e: e.activation(out=v3(yv[:, c, :]), in_=bank3(0), func=AF.Sigmoid), reads=pskeys(0), writes=['yv'])
        load_win(2048)
        for c in range(4):
            proj_fm(c, 3)
            S.op('dve', lambda e: e.tensor_tensor(out=hxb[:, c, 30:30 + 768].rearrange("p (a b) -> p a b", b=384),
                                                  in0=psum[:, 3:5, 0:384], in1=yv[:, c, 0:768].rearrange("p (a b) -> p a b", b=384),
                                                  op=ALU.mult), reads=['ps3', 'ps4', 'yv'], writes=['hxb'])
            S.op('dve', lambda e: e.tensor_tensor(out=hxb[:, c, 30 + 768:30 + NP], in0=psum[:, 5, 0:256], in1=yv[:, c, 768:NP],
                                                  op=ALU.mult), reads=['ps5', 'yv'], writes=['hxb'])
            S.op('dve', lambda e: e.tensor_tensor(out=hlast[:, c, :], in0=psum[:, 5, 226:256], in1=yv[:, c, NP - 30:NP],
                                                  op=ALU.mult), reads=['ps5', 'yv'], writes=['hlast'])
            S.op('dve', lambda e: e.tensor_tensor(out=hxs[:, c, :, 30:38], in0=psum[:, 5, 256:384].rearrange("p (n t) -> p n t", t=8),
                                                  in1=yv[:, c, NP:NT].rearrange("p (n t) -> p n t", t=8), op=ALU.mult),
                 reads=['ps5', 'yv'], writes=['hxs'])
        S.barrier()
        for c in range(4):
            for j in range(31):
                S.op('dve', lambda e: e.tensor_scalar(out=diag[:, j, :], in0=ident[:], scalar1=cw[:, c, j:j + 1], scalar2=0.0,
                                                      op0=ALU.mult, op1=ALU.add), reads=['ident', 'cw'], writes=['diag'])
            for hf in range(2):
                bk = 4 + hf
                for j in range(31):
                    S.op('pe', lambda e: e.matmul(psum[:, bk, :], lhsT=diag[:, j, :], rhs=hxb[:, c, j + hf * 512:j + hf * 512 + 512],
                                                  start=(j == 0), stop=(j == 30)), reads=['diag', 'hxb'], writes=['ps%d' % bk],
                         sig=(j == 30))
                S.op('act', lambda e: e.activation(out=yv[:, c, hf * 512:(hf + 1) * 512], in_=psum[:, bk, :], func=AF.Identity,
                                                   bias=pv(l, 'conv_b', c)), reads=['ps%d' % bk, 'pvec'], writes=['yv%d' % c])
        for c in range(4):
            ys = yv[:, c, NP:NT].rearrange("p (n t) -> p n t", t=8)
            S.op('dve', lambda e: e.tensor_scalar(out=ys, in0=hxs[:, c, :, 0:8], scalar1=cw[:, c, 0:1], scalar2=pv(l, 'conv_b', c),
                                                  op0=ALU.mult, op1=ALU.add), reads=['hxs', 'cw', 'pvec'], writes=['yvs%d' % c])
            for j in range(1, 31):
                S.op('dve', lambda e: e.scalar_tensor_tensor(out=ys, in0=hxs[:, c, :, j:j + 8], scalar=cw[:, c, j:j + 1], in1=ys,
                                                             op0=ALU.mult, op1=ALU.add), reads=['hxs', 'yvs%d' % c], writes=['yvs%d' % c])
        S.barrier()
        tr_out(lambda c: hlast[:, c, :], 30, o_conv[l, 16], 'hlast', sgt[:, 0:512], ccst)
        for hs in range(4):
            tr_out(lambda c: hxs[:, c, 4 * hs:4 * hs + 4, 8:38], 120, o_conv[l, 4 * hs:4 * hs + 4].rearrange("n r c -> (n r) c"), 'hxs',
                   sgt[:, 0:512], ccst)
        S.barrier()
        mean = reg[:, 0:NT]
        rstd = reg[:, NT:2 * NT]
        wpw = reg[:, 2 * NT:2 * NT + 1024].bitcast(BF16).rearrange("p (k d) -> p k d", d=512)
        sl = hxs.rearrange("p c n t -> p (c n t)")[:, 0:2304].bitcast(BF16).rearrange("p (c t) -> p c t", t=NT)
        ysq = sgt
        ones_sum(lambda c: yv[:, c, :], 4, lambda c: 'yvf', f32=True)
        S.op('act', lambda e: e.activation(out=v3(mean), in_=bank3(0), func=AF.Copy, scale=1.0 / 512), reads=pskeys(0), writes=['cmean'])
        for c in range(4):
            S.op('act', lambda e: e.activation(out=ysq, in_=yv[:, c, :], func=AF.Square), reads=['yvf'], writes=['ysq'])
            for tb, (a, b) in enumerate(TB):
                S.op('pe', lambda e: e.matmul(psum[:, 3 + tb, 0:384], lhsT=ones_f[:], rhs=ysq[:, a:b], start=(c == 0), stop=(c == 3)),
                     reads=['ysq', 'ones_f'], writes=['ps%d' % (3 + tb)], sig=(tb == 2))
        S.barrier()
        S.op('dve', lambda e: e.tensor_tensor(out=sgt, in0=mean, in1=mean, op=ALU.mult), reads=['cmean'], writes=['sgt'])
        S.op('dve', lambda e: e.scalar_tensor_tensor(out=v3(rstd), in0=bank3(3), scalar=1.0 / 512, in1=v3(sgt), op0=ALU.mult, op1=ALU.subtract),
             reads=pskeys(3) + ['sgt'], writes=['crstd'])
        S.op('act', lambda e: e.activation(out=rstd, in_=rstd, func=AF.Sqrt, bias=EPS), reads=['crstd'], writes=['crstd'])
        S.op('dve', lambda e: e.reciprocal(out=rstd, in_=rstd), reads=['crstd'], writes=['crstd'])
        S.dma('pool', wpw, W['conv_w_pw'][l].rearrange("(k p) d -> p k d", p=128), writes=['wpw'], sem='wpw')
        for c in range(4):
            S.op('dve', lambda e: e.tensor_tensor(out=yv[:, c, :], in0=yv[:, c, :], in1=mean, op=ALU.subtract), reads=['yvf', 'cmean'], writes=['yvf'])
            S.op('dve', lambda e: e.tensor_tensor(out=yv[:, c, :], in0=yv[:, c, :], in1=rstd, op=ALU.mult), reads=['yvf', 'crstd'], writes=['yvf'])
            S.op('act', lambda e: e.activation(out=sl[:, c, :], in_=yv[:, c, :], func=AF.Silu, scale=pv(l, 'conv_ln_g', c),
                                               bias=pv(l, 'conv_ln_b', c)), reads=['yvf', 'pvec'], writes=['sl'])
        for m in range(4):
            for k in range(4):
                for tb, (a, b) in enumerate(TB):
                    S.op('pe', lambda e: e.matmul(psum[:, tb, 0:384], lhsT=wpw[:, k, m * 128:(m + 1) * 128], rhs=sl[:, k, a:b],
                                                  start=(k == 0), stop=(k == 3)), reads=['wpw', 'sl'], writes=['ps%d' % tb], sig=(k == 3 and tb == 2))
            evac(v3(om[:, m, :]), bank3(0), pskeys(0), ['om'])
        S.barrier()
        AR.off = m0
        finish(3)

    if isinstance(dbg, tuple):
        mixers(0)
    else:
        for l in range(2):
            ffn(l, 1)
            if dbg != 'ffn_only':
                mixers(l)
            ffn(l, 2)
    final_out()
    return nc


def _consts(core):
    p = np.arange(128)
    ident = np.eye(128, dtype=np.float32)
    tri = (p[:, None] <= p[None, :]).astype(np.float32)
    mblk = ((p[:, None] // 8 == p[None, :] // 8) & (p[:, None] % 8 <= p[None, :] % 8)).astype(np.float32)
    e8 = (np.arange(8)[:, None] == (p[None, :] % 8)).astype(np.float32)
    pos0 = (core % 2) * 1024
    posb = np.broadcast_to((pos0 + np.arange(16)).astype(np.float32)[None, :], (128, 16)).copy()
    modd = np.full((128, 1), float(core % 2), np.float32)
    tv = np.broadcast_to((np.arange(128) + 1).astype(np.float32)[None, :], (128, 128)).copy()
    g2m = (((p[:, None] // 16) % 2) == np.arange(2)[None, :]).astype(np.float32)
    rm = ((p[:, None] // 32) == np.arange(4)[None, :]).astype(np.float32)
    return dict(c_ident=ident, c_tri=tri, c_mblk=mblk, c_e8=e8, c_posb=posb, c_modd=modd, c_tv=tv, c_g2m=g2m, c_rm=rm)


_NC_CACHE = {}
DBG = None


def kernel(**inputs):
    x_prompt = np.asarray(inputs['x_prompt'], np.float32)
    x_sample = np.asarray(inputs['x_sample'], np.float32)
    if 'nc' not in _NC_CACHE:
        _NC_CACHE["nc"] = build_program(DBG)
    nc = _NC_CACHE['nc']
    in_maps = []
    for c in range(8):
        b, half = c // 2, c % 2
        xp = x_prompt[b, half * 1024:(half + 1) * 1024]
        xs = x_sample[16 * c:16 * c + 16].reshape(128, D)
        m = {'xtok': np.ascontiguousarray(np.concatenate([xp, xs], axis=0)),
             'st_pool': np.ascontiguousarray(inputs['state_pool'][:, 16 * c:16 * c + 16]),
             'st_conv': np.ascontiguousarray(inputs['state_conv'][:, 16 * c:16 * c + 16]),
             'st_re': np.ascontiguousarray(inputs['state_ssm_re'][:, 16 * c:16 * c + 16]).reshape(2, 16, 2048),
             'st_im': np.ascontiguousarray(inputs['state_ssm_im'][:, 16 * c:16 * c + 16]).reshape(2, 16, 2048)}
        for n in WNAMES:
            m[n] = (np.zeros((2, 128, 128), np.float32) if (isinstance(DBG, tuple) and n.startswith('ffn') and 'norm' not in n)
                    else np.asarray(inputs[n], np.float32))
        m.update(_consts(c))
        in_maps.append(m)
    res = run_bass_kernel_spmd(nc, in_maps, core_ids=list(range(8)))
    R = res.results
    y_prompt = np.zeros((4, 2048, D), np.float32)
    y_sample = np.zeros((128, 8, D), np.float32)
    pool_p = np.zeros((2, 4, 15, 512), np.float32)
    pool_s = np.zeros((2, 128, 15, 512), np.float32)
    conv_p = np.zeros((2, 4, 30, 512), np.float32)
    conv_s = np.zeros((2, 128, 30, 512), np.float32)
    re_p = np.zeros((2, 4, 32, 64), np.float32)
    im_p = np.zeros((2, 4, 32, 64), np.float32)
    re_s = np.zeros((2, 128, 32, 64), np.float32)
    im_s = np.zeros((2, 128, 32, 64), np.float32)
    v_s = np.zeros((2, 128, 8, 512), np.float32)
    for c in range(8):
        b, half = c // 2, c % 2
        r = R[c]
        y_prompt[b, half * 1024:(half + 1) * 1024] = r['o_y'][:1024]
        y_sample[16 * c:16 * c + 16] = r['o_y'][1024:].reshape(16, 8, D)
        pool_s[:, 16 * c:16 * c + 16] = r['o_pool'][:, :16]
        conv_s[:, 16 * c:16 * c + 16] = r['o_conv'][:, :16]
        re_s[:, 16 * c:16 * c + 16] = r['o_re'][:, :16].reshape(2, 16, 32, 64)
        im_s[:, 16 * c:16 * c + 16] = r['o_im'][:, :16].reshape(2, 16, 32, 64)
        v_s[:, 16 * c:16 * c + 16] = r['o_sgv'].reshape(2, 16, 8, 512)
        if half == 1:
            pool_p[:, b] = r['o_pool'][:, 16]
            conv_p[:, b] = r['o_conv'][:, 16]
            re_p[:, b] = r['o_re'][:, 16].reshape(2, 32, 64)
            im_p[:, b] = r['o_im'][:, 16].reshape(2, 32, 64)
    return (y_prompt, y_sample, pool_p, pool_s, conv_p, conv_s, re_p, im_p, re_s, im_s, v_s)
```
